# Optimizing a Trainium2 kernel written in Bass

```python
import math
import jax, jax.numpy as jnp
from jax import lax
import numpy as np

D_MODEL = 2048
BATCH = 4
SEQ = 2048
DEPTH = 1
DEC_BATCH = 8
DEC_SEQ = 32
PAST_LEN = 1024

CHUNK = 64
QBLK = 128
N_MEM = 256
EPS = 1e-6

A_HEADS = 8
A_KV_HEADS = 2
A_HEAD_DIM = 128
A_WIDTH = A_HEADS * A_HEAD_DIM
A_KV_WIDTH = A_KV_HEADS * A_HEAD_DIM
IDX_HEADS = 16
IDX_DIM = 64
TOPK_MAX = 256
IDX_SCALE = (IDX_DIM * IDX_HEADS) ** -0.5

B_WIDTH = 2048
B_HEAD_DIM = 64
B_HEADS = B_WIDTH // B_HEAD_DIM
B_GROUPS = 4
B_HPG = B_HEADS // B_GROUPS
B_STATE = 128
B_CONV = 4
B_CONV_DIM = B_WIDTH + 2 * B_GROUPS * B_STATE

M_HEADS = 4
M_HEAD_DIM = 256
M_WIDTH = M_HEADS * M_HEAD_DIM

N_BRANCH = 3
IN_SPLITS = (A_WIDTH, A_KV_WIDTH, A_KV_WIDTH, IDX_HEADS * IDX_DIM, IDX_DIM, IDX_HEADS, A_WIDTH,
             B_WIDTH, B_CONV_DIM, B_HEADS, M_WIDTH, M_WIDTH, N_BRANCH * D_MODEL)
IN_COLS = sum(IN_SPLITS)

kernel_name = 'dsa_ssd_gated_parallel_streaming_step'


def rmsnorm(x, g):
    xf = x.astype(jnp.float32)
    y = xf * lax.rsqrt(jnp.mean(xf * xf, axis=-1, keepdims=True) + EPS)
    return (y * g.astype(jnp.float32)).astype(x.dtype)


def split_cols(u):
    idx, acc = [], 0
    for w in IN_SPLITS[:-1]:
        acc += w
        idx.append(acc)
    return jnp.split(u, idx, axis=-1)


def dsa_block(q, qi, wi, qpos, k, v, ki, kpos, topk):
    f32 = jnp.float32
    bsz, nq = q.shape[:2]
    rel = jax.nn.relu(jnp.einsum('bqhd,bsd->bqhs', qi.astype(f32), ki.astype(f32)))
    score = jnp.einsum('bqhs,bqh->bqs', rel, wi.astype(f32)) * IDX_SCALE
    adm = (kpos[None, :] // CHUNK) <= (qpos[:, None] // CHUNK)
    score = jnp.where(adm[None], score, -jnp.inf)
    vals, idx = lax.top_k(score, topk)
    valid = jnp.isfinite(vals)
    gather = jax.vmap(lambda t, i: t[i])
    kg = gather(k, idx).astype(f32)
    vg = gather(v, idx).astype(f32)
    qg = q.reshape(bsz, nq, A_KV_HEADS, A_HEADS // A_KV_HEADS, A_HEAD_DIM).astype(f32)
    s = jnp.einsum('bqhgd,bqkhd->bqhgk', qg, kg) * (A_HEAD_DIM ** -0.5)
    s = jnp.where(valid[:, :, None, None, :], s, -jnp.inf)
    p = jax.nn.softmax(s, axis=-1)
    o = jnp.einsum('bqhgk,bqkhd->bqhgd', p, vg)
    return o.reshape(bsz, nq, A_WIDTH).astype(q.dtype)


def dsa_attention(q, qi, wi, k, v, ki, pos0):
    bsz, T = q.shape[:2]
    L = k.shape[1]
    topk = min(TOPK_MAX, L // 4)
    blk = min(QBLK, T)
    nb = T // blk
    kpos = jnp.arange(L)
    qpos = (pos0 + jnp.arange(T)).reshape(nb, blk)

    def blocks(t):
        return jnp.moveaxis(t.reshape((bsz, nb, blk) + t.shape[2:]), 1, 0)

    def one_block(args):
        qb, qib, wib, qpb = args
        return dsa_block(qb, qib, wib, qpb, k, v, ki, kpos, topk)

    out = lax.map(one_block, (blocks(q), blocks(qi), blocks(wi), qpos))
    return jnp.moveaxis(out, 0, 1).reshape(bsz, T, A_WIDTH)


def causal_dwconv(u, prev, w, b):
    up = jnp.concatenate([prev, u], axis=1)
    y = lax.conv_general_dilated(up, w[:, None, :], window_strides=(1,), padding='VALID',
                                 dimension_numbers=('NWC', 'WIO', 'NWC'),
                                 feature_group_count=u.shape[-1])
    return y + b, up[:, up.shape[1] - (B_CONV - 1):]


def ssd_scan(xh, dt, a, bm, cm, h0):
    f32 = jnp.float32
    bsz, T = xh.shape[:2]
    l = min(CHUNK, T)
    c = T // l
    x = xh.reshape(bsz, c, l, B_GROUPS, B_HPG, B_HEAD_DIM).astype(f32)
    dtc = dt.reshape(bsz, c, l, B_GROUPS, B_HPG)
    bc = bm.reshape(bsz, c, l, B_GROUPS, B_STATE).astype(f32)
    cc = cm.reshape(bsz, c, l, B_GROUPS, B_STATE).astype(f32)
    cum = jnp.cumsum(dtc * a.reshape(B_GROUPS, B_HPG), axis=2)
    causal = jnp.tril(jnp.ones((l, l), dtype=bool))[:, :, None, None]
    seg = jnp.where(causal, cum[:, :, :, None] - cum[:, :, None, :], -jnp.inf)
    cb = jnp.einsum('bclgn,bcsgn->bclsg', cc, bc)
    wgt = cb[..., None] * jnp.exp(seg) * dtc[:, :, None]
    y_diag = jnp.einsum('bclsgr,bcsgrp->bclgrp', wgt, x)
    decay_to_end = jnp.exp(cum[:, :, -1:] - cum) * dtc
    states = jnp.einsum('bclgn,bclgr,bclgrp->bcgrpn', bc, decay_to_end, x)
    chunk_decay = jnp.exp(cum[:, :, -1])

    def step(h, inp):
        s_c, d_c = inp
        return h * d_c[..., None, None] + s_c, h

    h_init = h0.reshape(bsz, B_GROUPS, B_HPG, B_HEAD_DIM, B_STATE).astype(f32)
    h_last, h_in = lax.scan(step, h_init, (jnp.moveaxis(states, 1, 0), jnp.moveaxis(chunk_decay, 1, 0)))
    h_in = jnp.moveaxis(h_in, 0, 1)
    y_off = jnp.einsum('bclgn,bcgrpn,bclgr->bclgrp', cc, h_in, jnp.exp(cum))
    y = (y_diag + y_off).reshape(bsz, T, B_HEADS, B_HEAD_DIM)
    return y, h_last.reshape(bsz, B_HEADS, B_HEAD_DIM, B_STATE)


def mamba_branch(z, xbc, dt_raw, conv_prev, h0, conv_w, conv_b, dt_bias, a_log, d_skip, ssm_norm):
    f32 = jnp.float32
    bsz, T = z.shape[:2]
    xbc, conv_state = causal_dwconv(xbc, conv_prev, conv_w, conv_b)
    xbc = jax.nn.silu(xbc)
    gn = B_GROUPS * B_STATE
    xs = xbc[..., :B_WIDTH]
    bm = xbc[..., B_WIDTH:B_WIDTH + gn].reshape(bsz, T, B_GROUPS, B_STATE)
    cm = xbc[..., B_WIDTH + gn:].reshape(bsz, T, B_GROUPS, B_STATE)
    dt = jax.nn.softplus(dt_raw.astype(f32) + dt_bias.astype(f32))
    a = -jnp.exp(a_log.astype(f32))
    xh = xs.reshape(bsz, T, B_HEADS, B_HEAD_DIM)
    y, h_last = ssd_scan(xh, dt, a, bm, cm, h0)
    y = y + d_skip.astype(f32)[:, None] * xh.astype(f32)
    gsz = B_WIDTH // B_GROUPS
    y = y.reshape(bsz, T, B_GROUPS, gsz) * jax.nn.silu(z.astype(f32)).reshape(bsz, T, B_GROUPS, gsz)
    y = y * lax.rsqrt(jnp.mean(y * y, axis=-1, keepdims=True) + EPS)
    y = y.reshape(bsz, T, B_WIDTH) * ssm_norm.astype(f32)
    return y.astype(z.dtype), conv_state, h_last.astype(h0.dtype)


def mem_attend(q, mk, mv):
    f32 = jnp.float32
    bsz, T = q.shape[:2]
    qh = q.reshape(bsz, T, M_HEADS, M_HEAD_DIM).astype(f32)
    s = jnp.einsum('bthd,bmhd->bhtm', qh, mk.astype(f32)) * (M_HEAD_DIM ** -0.5)
    p = jax.nn.softmax(s, axis=-1)
    o = jnp.einsum('bhtm,bmhd->bthd', p, mv.astype(f32))
    return o.reshape(bsz, T, M_WIDTH).astype(q.dtype)


def trunk_layer(x, pos0, past_k, past_v, past_ki, conv_prev, h0, mem_k, mem_v,
                norm_in, w_in, conv_w, conv_b, dt_bias, a_log, d_skip, ssm_norm, w_pa, w_pb, w_pm, w_o):
    bsz, T, _ = x.shape
    h = rmsnorm(x, norm_in)
    (a_q, a_k, a_v, i_q, i_k, i_w, a_z, b_z, b_xbc, b_dt, m_q, m_z, gates) = split_cols(h @ w_in)
    k_new = a_k.reshape(bsz, T, A_KV_HEADS, A_HEAD_DIM)
    v_new = a_v.reshape(bsz, T, A_KV_HEADS, A_HEAD_DIM)
    k_all = jnp.concatenate([past_k, k_new], axis=1)
    v_all = jnp.concatenate([past_v, v_new], axis=1)
    ki_all = jnp.concatenate([past_ki, i_k], axis=1)
    ya = dsa_attention(a_q.reshape(bsz, T, A_HEADS, A_HEAD_DIM), i_q.reshape(bsz, T, IDX_HEADS, IDX_DIM),
                       i_w, k_all, v_all, ki_all, pos0)
    ya = ya * jax.nn.silu(a_z)
    yb, conv_state, h_last = mamba_branch(b_z, b_xbc, b_dt, conv_prev, h0, conv_w, conv_b,
                                          dt_bias, a_log, d_skip, ssm_norm)
    ym = mem_attend(m_q, mem_k, mem_v) * jax.nn.silu(m_z)
    g_a, g_b, g_m = jnp.split(jax.nn.sigmoid(gates), N_BRANCH, axis=-1)
    merged = g_a * (ya @ w_pa) + g_b * (yb @ w_pb) + g_m * (ym @ w_pm)
    return x + merged @ w_o, (k_new, v_new, i_k, conv_state, h_last)


def setup_inputs(seed: int = 0) -> dict:
    key = jax.random.key(seed)
    ks = jax.random.split(key, 32)
    f32 = jnp.float32

    def nrm(k, shape, scale):
        return scale * jax.random.normal(k, shape, f32)

    def gain(k, shape):
        return 1.0 + 0.02 * jax.random.normal(k, shape, f32)

    dt0 = jnp.exp(jax.random.uniform(ks[20], (DEPTH, B_HEADS), f32, math.log(1e-3), math.log(1e-1)))
    return {
        'x_prompt': nrm(ks[0], (BATCH, SEQ, D_MODEL), 1.0),
        'x_sample': nrm(ks[1], (DEC_BATCH, DEC_SEQ, D_MODEL), 1.0),
        'mem_prompt': nrm(ks[2], (BATCH, N_MEM, D_MODEL), 1.0),
        'cache_attn_k': nrm(ks[3], (DEPTH, DEC_BATCH, PAST_LEN, A_KV_HEADS, A_HEAD_DIM), 1.0),
        'cache_attn_v': nrm(ks[4], (DEPTH, DEC_BATCH, PAST_LEN, A_KV_HEADS, A_HEAD_DIM), 1.0),
        'cache_idx_k': nrm(ks[5], (DEPTH, DEC_BATCH, PAST_LEN, IDX_DIM), 1.0),
        'state_conv': nrm(ks[6], (DEPTH, DEC_BATCH, B_CONV - 1, B_CONV_DIM), 1.0),
        'state_ssm': nrm(ks[7], (DEPTH, DEC_BATCH, B_HEADS, B_HEAD_DIM, B_STATE), 0.1),
        'cache_mem_k': nrm(ks[8], (DEPTH, DEC_BATCH, N_MEM, M_HEADS, M_HEAD_DIM), 1.0),
        'cache_mem_v': nrm(ks[9], (DEPTH, DEC_BATCH, N_MEM, M_HEADS, M_HEAD_DIM), 1.0),
        'norm_in': gain(ks[10], (DEPTH, D_MODEL)),
        'w_in': nrm(ks[11], (DEPTH, D_MODEL, IN_COLS), D_MODEL ** -0.5),
        'conv_w': nrm(ks[12], (DEPTH, B_CONV, B_CONV_DIM), B_CONV ** -0.5),
        'conv_b': nrm(ks[13], (DEPTH, B_CONV_DIM), 0.02),
        'dt_bias': dt0 + jnp.log(-jnp.expm1(-dt0)),
        'a_log': jnp.log(jax.random.uniform(ks[14], (DEPTH, B_HEADS), f32, 1.0, 16.0)),
        'd_skip': gain(ks[15], (DEPTH, B_HEADS)),
        'ssm_norm': gain(ks[16], (DEPTH, B_WIDTH)),
        'norm_mem': gain(ks[17], (DEPTH, D_MODEL)),
        'w_mem_kv': nrm(ks[18], (DEPTH, D_MODEL, 2 * M_WIDTH), D_MODEL ** -0.5),
        'w_pa': nrm(ks[19], (DEPTH, A_WIDTH, D_MODEL), A_WIDTH ** -0.5),
        'w_pb': nrm(ks[21], (DEPTH, B_WIDTH, D_MODEL), B_WIDTH ** -0.5),
        'w_pm': nrm(ks[22], (DEPTH, M_WIDTH, D_MODEL), M_WIDTH ** -0.5),
        'w_o': nrm(ks[23], (DEPTH, D_MODEL, D_MODEL), D_MODEL ** -0.5),
        'norm_final': gain(ks[24], (D_MODEL,)),
    }


def reference(x_prompt, x_sample, mem_prompt, cache_attn_k, cache_attn_v, cache_idx_k, state_conv, state_ssm,
              cache_mem_k, cache_mem_v, norm_in, w_in, conv_w, conv_b, dt_bias, a_log, d_skip, ssm_norm,
              norm_mem, w_mem_kv, w_pa, w_pb, w_pm, w_o, norm_final):
    xp, xs = x_prompt, x_sample
    bp = xp.shape[0]
    new_p, new_s = [], []
    for l in range(DEPTH):
        lw = (norm_in[l], w_in[l], conv_w[l], conv_b[l], dt_bias[l], a_log[l], d_skip[l], ssm_norm[l],
              w_pa[l], w_pb[l], w_pm[l], w_o[l])
        mkv = rmsnorm(mem_prompt, norm_mem[l]) @ w_mem_kv[l]
        mk_p = mkv[..., :M_WIDTH].reshape(bp, N_MEM, M_HEADS, M_HEAD_DIM)
        mv_p = mkv[..., M_WIDTH:].reshape(bp, N_MEM, M_HEADS, M_HEAD_DIM)
        empty_kv = jnp.zeros((bp, 0, A_KV_HEADS, A_HEAD_DIM), xp.dtype)
        empty_ki = jnp.zeros((bp, 0, IDX_DIM), xp.dtype)
        conv0 = jnp.zeros((bp, B_CONV - 1, B_CONV_DIM), xp.dtype)
        h0 = jnp.zeros((bp, B_HEADS, B_HEAD_DIM, B_STATE), xp.dtype)
        xp, st_p = trunk_layer(xp, 0, empty_kv, empty_kv, empty_ki, conv0, h0, mk_p, mv_p, *lw)
        xs, st_s = trunk_layer(xs, PAST_LEN, cache_attn_k[l], cache_attn_v[l], cache_idx_k[l], state_conv[l],
                               state_ssm[l], cache_mem_k[l], cache_mem_v[l], *lw)
        new_p.append(st_p + (mk_p, mv_p))
        new_s.append(st_s)
    y_prompt = rmsnorm(xp, norm_final)
    y_sample = rmsnorm(xs, norm_final)
    sp = [jnp.stack(t) for t in zip(*new_p)]
    ss = [jnp.stack(t) for t in zip(*new_s)]
    return (y_prompt, y_sample, sp[0], sp[1], sp[2], sp[3], sp[4], sp[5], sp[6], ss[0], ss[1], ss[2], ss[3], ss[4])
```

```python
import math
from contextlib import ExitStack

import numpy as np
import concourse.bass as bass
import concourse.mybir as mybir
from concourse.bass_utils import run_bass_kernel_spmd

F32 = mybir.dt.float32
BF16 = mybir.dt.bfloat16
ALU = mybir.AluOpType
AF = mybir.ActivationFunctionType

D = 2048
TO = 1024
TS = 32
TT = TO + TS
TC = 1024
NM = 256
EPS = 1e-6
NEG = -30000.0
TOPK = 256
COL = dict(a_q=0, a_k=1024, a_v=1280, i_q=1536, i_k=2560, i_w=2624, a_z=2640, b_z=3664,
           b_xbc=5712, b_dt=8784, m_q=8816, m_z=9840, gates=10864)
NBIS = 30
DEBUG = False
PHASE_LIMIT = 99
ATT_STAGES = 4
ATT_PRO = 9
IDX_ENG = "dve"
BIS_LO = -4096.0
BIS_W = 8192.0


class Res:
    __slots__ = ("name", "w", "rs")

    def __init__(self, name):
        self.name = name
        self.w = None
        self.rs = {}


class Sched:
    COMPUTE = ("pe", "act", "dve", "pool")
    ALL = ("pe", "act", "dve", "pool", "sp")

    def __init__(self, nc, ndma_sems=(("sp", 24), ("pool", 8))):
        self.nc = nc
        self.ops = {e: [] for e in self.ALL}
        self.cnt = {e: 0 for e in self.COMPUTE}
        self.seen = {e: {} for e in self.ALL}
        self.dma_slots = {q: [[f"dma_{q}_{i}", 0] for i in range(n)] for q, n in ndma_sems}
        self.dma_rr = {q: 0 for q, _ in ndma_sems}
        self.sems = {}
        self.fence_ev = {}

    def fence(self):
        f = {e: c for e, c in self.cnt.items() if c > 0}
        for q, slots in self.dma_slots.items():
            for s in slots:
                if s[1] > 0:
                    f[s[0]] = s[1]
        self.fence_ev = f

    def _deps(self, eng, reads, writes):
        deps = {}

        def add(k, v):
            if k == eng and eng == "pe":
                return
            if self.seen[eng].get(k, 0) >= v:
                return
            if deps.get(k, 0) < v:
                deps[k] = v
        if eng != "pool":
            for k, v in self.fence_ev.items():
                add(k, v)
        for r in reads:
            if r.w is not None:
                add(*r.w)
        for w in writes:
            if w.w is not None:
                add(*w.w)
            for k, v in w.rs.items():
                add(k, v)
        return deps

    def _commit(self, eng, deps, ev, reads, writes):
        for k, v in deps.items():
            self.seen[eng][k] = v
        k, v = ev
        for r in reads:
            if r.rs.get(k, 0) < v:
                r.rs[k] = v
        for w in writes:
            w.w = ev
            w.rs = {}

    def op(self, eng, fn, reads=(), writes=()):
        deps = self._deps(eng, reads, writes)
        self.cnt[eng] += 1
        ev = (eng, self.cnt[eng])
        self.ops[eng].append((list(deps.items()), fn, ev, 1))
        self._commit(eng, deps, ev, reads, writes)
        return ev

    def dma(self, q, fn, reads=(), writes=()):
        slots = self.dma_slots[q]
        i = self.dma_rr[q]
        self.dma_rr[q] = (i + 1) % len(slots)
        slot = slots[i]
        deps = self._deps(q, reads, writes)
        if slot[1] > 0 and self.seen[q].get(slot[0], 0) < slot[1]:
            if deps.get(slot[0], 0) < slot[1]:
                deps[slot[0]] = slot[1]
        slot[1] += 16
        ev = (slot[0], slot[1])
        self.ops[q].append((list(deps.items()), fn, ev, 16))
        self._commit(q, deps, ev, reads, writes)
        return ev

    def emit(self, stack):
        nc = self.nc
        names = list(self.COMPUTE)
        for q, slots in self.dma_slots.items():
            names += [s[0] for s in slots]
        for n in names:
            self.sems[n] = stack.enter_context(nc.semaphore(n))
        block = stack.enter_context(nc.Block())
        final = []
        for q, slots in self.dma_slots.items():
            for s in slots:
                if s[1] > 0:
                    final.append((s[0], s[1]))
        for e in self.COMPUTE:
            if self.cnt[e] > 0:
                final.append((e, self.cnt[e]))
        needed = {e: set() for e in self.COMPUTE}
        for e in self.ALL:
            for waits, fn, ev, inc in self.ops[e]:
                for k, v in waits:
                    if k in needed:
                        needed[k].add(v)
        for k, v in final:
            if k in needed:
                needed[k].add(v)
        remap = {}
        for e in self.COMPUTE:
            remap[e] = {v: i + 1 for i, v in enumerate(sorted(needed[e]))}

        def tr(k, v):
            return remap[k][v] if k in remap else v

        def replay(eng_name, extra_final=False):
            def body(h):
                for waits, fn, ev, inc in self.ops[eng_name]:
                    for k, v in waits:
                        h.wait_ge(self.sems[k], tr(k, v))
                    ins = fn(h)
                    if ev[0] not in remap or ev[1] in remap[ev[0]]:
                        ins.then_inc(self.sems[ev[0]], inc)
                if extra_final:
                    for k, v in final:
                        h.wait_ge(self.sems[k], tr(k, v))
            return body

        block.tensor(replay("pe"))
        block.scalar(replay("act"))
        block.vector(replay("dve"))
        block.gpsimd(replay("pool"))
        block.sync(replay("sp", extra_final=True))


class Arena:
    def __init__(self, nc, st, nbytes):
        self.t = st.enter_context(nc.sbuf_tensor("arena", [128, nbytes // 2], BF16))
        self.off = 0
        self.cap = nbytes
        self.k = 0

    def alloc(self, dims, dtype, name=None):
        esz = 4 if dtype == F32 else 2
        n = 1
        for d_ in dims:
            n *= d_
        nb = (n * esz + 63) // 64 * 64
        assert self.off + nb <= self.cap, ("arena overflow", name, self.off, nb, self.cap)
        v = self.t[:, self.off // 2:(self.off + n * esz) // 2]
        self.off += nb
        if dtype == F32:
            v = v.bitcast(F32)
        if len(dims) == 2:
            v = v.rearrange("p (a b) -> p a b", a=dims[0])
        elif len(dims) == 3:
            v = v.rearrange("p (a b c) -> p a b c", a=dims[0], b=dims[1])
        self.k += 1
        return v, Res(name or f"buf{self.k}")

    def mark(self):
        return self.off

    def release(self, m):
        self.off = m


def build_program():
    nc = bass.Bass("TRN2", target_bir_lowering=False)
    I = lambda n, s: nc.dram_tensor(n, list(s), F32, kind="ExternalInput").ap()
    O = lambda n, s: nc.dram_tensor(n, list(s), F32, kind="ExternalOutput").ap()
    x_own = I("x_own", (TO, D)); x_ctx = I("x_ctx", (TC, D)); x_smp = I("x_smp", (TS, D))
    memx = I("memx", (NM, D)); flags = I("flags", (128, 2))
    ck = I("ck", (1024, 256)); cv = I("cv", (1024, 256)); cki = I("cki", (1024, 64))
    sconv = I("sconv", (3, 3072)); sssm = I("sssm", (2048, 128))
    cmk = I("cmk", (NM, 1024)); cmv = I("cmv", (NM, 1024))
    norm_in = I("norm_in", (D,)); w_in = I("w_in", (D, 17008)); conv_w = I("conv_w", (4, 3072))
    conv_b = I("conv_b", (3072,)); dt_bias = I("dt_bias", (32,)); a_log = I("a_log", (32,))
    d_skip = I("d_skip", (32,)); ssm_norm = I("ssm_norm", (D,)); norm_mem = I("norm_mem", (D,))
    w_mem_kv = I("w_mem_kv", (D, 2048)); w_pa = I("w_pa", (1024, D)); w_pb = I("w_pb", (D, D))
    w_pm = I("w_pm", (1024, D)); w_o = I("w_o", (D, D)); norm_final = I("norm_final", (D,))

    y_own = O("y_own", (TO, D)); y_smp = O("y_smp", (TS, D))
    k_own = O("k_own", (TO, 256)); v_own = O("v_own", (TO, 256)); ki_own = O("ki_own", (TO, 64))
    conv_p = O("conv_p", (3, 3072)); ssm_p = O("ssm_p", (2048, 128))
    memk_o = O("memk_o", (NM, 1024)); memv_o = O("memv_o", (NM, 1024))
    k_s = O("k_s", (TS, 256)); v_s = O("v_s", (TS, 256)); ki_s = O("ki_s", (TS, 64))
    conv_s = O("conv_s", (3, 3072)); ssm_s = O("ssm_s", (2048, 128))
    zs_d = nc.dram_tensor("zs_d", [TT, D], BF16).ap()
    dR = {n: Res(n) for n in ["y_own", "y_smp", "k_own", "v_own", "ki_own", "conv_p", "ssm_p", "memk_o",
                              "memv_o", "k_s", "v_s", "ki_s", "conv_s", "ssm_s", "zs_d"]}

    with ExitStack() as st:
        S = Sched(nc)
        AR = Arena(nc, st, 212480)
        banks = []
        for i in range(8):
            b = st.enter_context(nc.psum_tensor(f"ps{i}", [128, 512], F32))
            banks.append((b, Res(f"ps{i}")))

        def bf(bank):
            return bank[:, :].bitcast(BF16)

        identf, identfR = AR.alloc([128], F32, "identf")
        ident, identR = AR.alloc([128], BF16, "ident")
        U, UR = AR.alloc([128], F32, "U")
        LT, LTR = AR.alloc([128], F32, "LT")
        onesf, onesfR = AR.alloc([128], F32, "onesf")
        onesb, onesbR = AR.alloc([128], BF16, "onesb")
        cmask, cmaskR = AR.alloc([128], F32, "cmask")
        IDm, IDmR = AR.alloc([32, 128], BF16, "IDm")
        gall, gallR = AR.alloc([48], F32, "gall")
        g_in, g_inR = gall[:, 0:16], gallR
        g_mem, g_memR = gall[:, 16:32], gallR
        g_ssm, g_ssmR = gall[:, 32:48], gallR
        cwall, cwallR = AR.alloc([120], F32, "cwall")
        cwR = cwallR
        cbR = cwallR
        s2all, s2allR = AR.alloc([72], F32, "s2all")
        vst, vstR = AR.alloc([128], F32, "vst")
        dtb, dtbR = AR.alloc([32], F32, "dtb")
        arow, arowR = AR.alloc([32], F32, "arow")
        dsk, dskR = AR.alloc([32], F32, "dsk")
        flg, flgR = AR.alloc([2], F32, "flg")
        sm, smR = AR.alloc([16], F32, "sm")

        S.op("pool", lambda h: h.memset(identf, 0.0), writes=[identfR])
        S.op("pool", lambda h: h.affine_select(out=identf, in_=identf, pattern=[[-1, 128]], compare_op=ALU.not_equal,
                                               fill=1.0, base=0, channel_multiplier=1), reads=[identfR], writes=[identfR])
        S.op("dve", lambda h: h.tensor_copy(out=ident, in_=identf), reads=[identfR], writes=[identR])
        S.op("pool", lambda h: h.memset(LT, 1.0), writes=[LTR])
        S.op("pool", lambda h: h.affine_select(out=LT, in_=LT, pattern=[[-1, 128]], compare_op=ALU.is_gt, fill=0.0,
                                               base=0, channel_multiplier=1), reads=[LTR], writes=[LTR])
        S.op("pool", lambda h: h.memset(U, 1.0), writes=[UR])
        S.op("pool", lambda h: h.affine_select(out=U, in_=U, pattern=[[1, 128]], compare_op=ALU.is_ge, fill=0.0,
                                               base=0, channel_multiplier=-1), reads=[UR], writes=[UR])
        S.op("pool", lambda h: h.memset(onesf, 1.0), writes=[onesfR])
        S.op("pool", lambda h: h.memset(onesb, 1.0), writes=[onesbR])
        S.op("pool", lambda h: h.memset(cmask, 0.0), writes=[cmaskR])
        S.op("pool", lambda h: h.memset(cmask[0:64, 64:128], NEG), reads=[cmaskR], writes=[cmaskR])
        def load_T(parts, dst, dstR):
            r = 0
            for src, n in parts:
                S.dma("sp", lambda h, src=src, r=r, n=n: h.dma_start(out=vst[r:r + n, :], in_=src), reads=[vstR], writes=[vstR])
                r += n
            pb, pbR = banks[7]
            S.op("pe", lambda h, r=r, pb=pb: h.transpose(out=pb[:, 0:r], in_=vst[0:r, :], identity=identf[0:r, 0:r]),
                 reads=[vstR, identfR], writes=[pbR])
            S.op("act", lambda h, r=r, pb=pb: h.activation(out=dst[:, 0:r], in_=pb[:, 0:r], func=AF.Copy),
                 reads=[pbR], writes=[dstR])
        load_T([(norm_in.rearrange("(c p) -> c p", p=128), 16), (norm_mem.rearrange("(c p) -> c p", p=128), 16),
                (ssm_norm.rearrange("(c p) -> c p", p=128), 16)], gall, gallR)
        load_T([(conv_w.rearrange("k (b p) -> (k b) p", p=128), 96), (conv_b.rearrange("(b p) -> b p", p=128), 24)],
               cwall, cwallR)
        load_T([(sconv.rearrange("k (b p) -> (k b) p", p=128), 72)], s2all, s2allR)
        S.dma("sp", lambda h: h.dma_start(out=dtb, in_=dt_bias.partition_broadcast(128)), writes=[dtbR])
        S.dma("sp", lambda h: h.dma_start(out=arow, in_=a_log.partition_broadcast(128)), writes=[arowR])
        S.dma("sp", lambda h: h.dma_start(out=dsk, in_=d_skip.partition_broadcast(128)), writes=[dskR])
        S.dma("sp", lambda h: h.dma_start(out=flg, in_=flags), writes=[flgR])
        S.op("act", lambda h: h.activation(out=arow, in_=arow, func=AF.Exp), reads=[arowR], writes=[arowR])
        S.op("dve", lambda h: h.tensor_scalar(out=arow, in0=arow, scalar1=-1.0, scalar2=None, op0=ALU.mult),
             reads=[arowR], writes=[arowR])
        S.op("dve", lambda h: h.tensor_tensor(out=IDm, in0=identf.unsqueeze(1).to_broadcast([128, 32, 128]),
                                              in1=dsk.unsqueeze(2).to_broadcast([128, 32, 128]), op=ALU.mult),
             reads=[identfR, dskR], writes=[IDmR])

        hT, hTR = AR.alloc([16, TT], BF16, "hT")
        wbufs = [AR.alloc([16, 256], BF16, f"wb{i}") for i in range(3)]
        wrr = [0]
        PH = AR.mark()

        def scratch(name, dims, dtype):
            if DEBUG and name in ("yaT_d", "ybT_d", "ymT_d"):
                return nc.dram_tensor(name, [128] + list(dims), dtype, kind="ExternalOutput").ap(), Res(name)
            return nc.dram_tensor(name, [128] + list(dims), dtype).ap(), Res(name)

        kT_d, kT_dR = scratch("kT_d", [2, TC], BF16)
        vS_d, vS_dR = scratch("vS_d", [8, 256], BF16)
        kiT_d, kiT_dR = scratch("kiT_d", [TC], BF16)
        hs_d, hs_dR = scratch("hs_d", [D], F32)
        tail_d, tail_dR = scratch("tail_d", [3, 24], F32)
        yaT_d, yaT_dR = scratch("yaT_d", [8, TT], BF16)
        ybT_d, ybT_dR = scratch("ybT_d", [16, TT], BF16)
        ymT_d, ymT_dR = scratch("ymT_d", [8, TT], BF16)

        def wnext():
            i = wrr[0]
            wrr[0] = (i + 1) % len(wbufs)
            return wbufs[i]

        def gemm(A, AR_, kc, W, wc0, wnc, mode, groups, epi, pbanks, post=None, dupcols=None, post_late=None):
            bi = 0
            late_q = []
            for c0 in range(wc0, wc0 + wnc, 256):
                ncol = min(256, wc0 + wnc - c0)
                wb, wbR = wnext()
                if dupcols is None:
                    S.dma("pool", lambda h, wb=wb, c0=c0, ncol=ncol: h.dma_start(
                        out=wb[:, 0:kc, 0:ncol], in_=W[:, c0:c0 + ncol].rearrange("(c p) n -> p c n", p=128)),
                        writes=[wbR])
                else:
                    for rep in range(dupcols):
                        S.dma("pool", lambda h, wb=wb, c0=c0, ncol=ncol, rep=rep: h.dma_start(
                            out=wb[:, 0:kc, rep * ncol:(rep + 1) * ncol],
                            in_=W[:, c0:c0 + ncol].rearrange("(c p) n -> p c n", p=128)), reads=[wbR], writes=[wbR])
                    ncol = ncol * dupcols
                if mode == "FM":
                    for s0 in range(0, ncol, 128):
                        ns = min(128, ncol - s0)
                        for (t0, t1) in groups:
                            pb, pbR = pbanks[bi % len(pbanks)]
                            bi += 1

                            def mm(h, wb=wb, s0=s0, ns=ns, t0=t0, t1=t1, pb=pb):
                                for k in range(kc):
                                    r = h.matmul(pb[0:ns, 0:t1 - t0], lhsT=wb[:, k, s0:s0 + ns], rhs=A[:, k, t0:t1],
                                                 start=(k == 0), stop=(k == kc - 1))
                                return r
                            S.op("pe", mm, reads=[wbR, AR_], writes=[pbR])
                            epi(pb[0:ns, 0:t1 - t0], pbR, c0 - wc0 + s0, ns, t0, t1)
                        if post is not None:
                            post(c0 - wc0 + s0, ns)
                        while late_q:
                            late_q.pop(0)()
                        if post_late is not None:
                            late_q.append(lambda a=c0 - wc0 + s0, b=ns: post_late(a, b))
                else:
                    for (t0, t1) in groups:
                        pb, pbR = pbanks[bi % len(pbanks)]
                        bi += 1

                        def mm(h, wb=wb, ncol=ncol, t0=t0, t1=t1, pb=pb):
                            for k in range(kc):
                                r = h.matmul(pb[0:t1 - t0, 0:ncol], lhsT=A[:, k, t0:t1], rhs=wb[:, k, 0:ncol],
                                             start=(k == 0), stop=(k == kc - 1))
                            return r
                        S.op("pe", mm, reads=[wbR, AR_], writes=[pbR])
                        epi(pb[0:t1 - t0, 0:ncol], pbR, c0 - wc0, ncol, t0, t1)
            while late_q:
                late_q.pop(0)()

        def tiles(t_lo, t_hi, step):
            return [(t, min(t + step, t_hi)) for t in range(t_lo, t_hi, step)]

        def norm_rows(src, nrows, gT, gTR, dstT, dstR, toff, tb):
            m = AR.mark()
            xts = [AR.alloc([D], F32, f"xt{i}") for i in range(2)]
            junk, junkR = AR.alloc([D], BF16, "junk")
            xn, xnR = AR.alloc([D], BF16, "xn")
            for i, r0 in enumerate(range(0, nrows, 128)):
                n = min(128, nrows - r0)
                xt, xtR = xts[i % 2]
                S.dma("sp", lambda h, xt=xt, r0=r0, n=n: h.dma_start(out=xt[0:n, :], in_=src[r0:r0 + n, :]),
                      writes=[xtR])
                S.op("act", lambda h, xt=xt, n=n: h.activation(out=junk[0:n, :], in_=xt[0:n, :], func=AF.Square,
                                                               scale=1.0 / math.sqrt(D), accum_out=sm[0:n, 0:1]),
                     reads=[xtR], writes=[junkR, smR])
                S.op("act", lambda h, n=n: h.activation(out=sm[0:n, 1:2], in_=sm[0:n, 0:1], func=AF.Ln, bias=EPS),
                     reads=[smR], writes=[smR])
                S.op("act", lambda h, n=n: h.activation(out=sm[0:n, 2:3], in_=sm[0:n, 1:2], func=AF.Exp, scale=-0.5), reads=[smR], writes=[smR])
                S.op("act", lambda h, xt=xt, n=n: h.activation(out=xn[0:n, :], in_=xt[0:n, :], func=AF.Copy,
                                                               scale=sm[0:n, 2:3]), reads=[xtR, smR], writes=[xnR])
                for half in range(2):
                    pb, pbR = tb[half % len(tb)]

                    def tr(h, n=n, half=half, pb=pb):
                        for j in range(8):
                            r = h.transpose(out=bf(pb)[:, j * 128:j * 128 + n],
                                            in_=xn[0:n, (half * 8 + j) * 128:(half * 8 + j + 1) * 128],
                                            identity=ident[0:n, 0:n])
                        return r
                    S.op("pe", tr, reads=[xnR, identR], writes=[pbR])
                    S.op("dve", lambda h, n=n, half=half, pb=pb, r0=r0: h.tensor_tensor(
                        out=dstT[:, half * 8:(half + 1) * 8, toff + r0:toff + r0 + n],
                        in0=bf(pb).rearrange("p (a b) -> p a b", a=8)[:, :, 0:n],
                        in1=gT[:, half * 8:(half + 1) * 8].unsqueeze(2).to_broadcast([128, 8, n]), op=ALU.mult),
                        reads=[pbR, gTR], writes=[dstR])
            AR.release(m)
            S.fence()

        B = banks

        def cp(eng, out, in_, reads, writes):
            if eng == "act":
                S.op("act", lambda h: h.activation(out=out, in_=in_, func=AF.Copy), reads=reads, writes=writes)
            else:
                S.op(eng, lambda h: h.tensor_copy(out=out, in_=in_), reads=reads, writes=writes)

        def ssm_pass(T, seqs, need_y, kind):
            m = AR.mark()
            ntile = (T + 127) // 128
            nseq = len(seqs)
            xTM, xTMR = AR.alloc([ntile, D], BF16, "xTM")
            BTM, BTMR = AR.alloc([ntile, 512], BF16, "BTM")
            BT, BTR = AR.alloc([4, T], BF16, "BT")
            CT, CTR = AR.alloc([4, T], BF16, "CT")
            dtv, dtvR = AR.alloc([ntile, 32], F32, "dtv")
            dta, dtaR = AR.alloc([ntile, 32], F32, "dta")
            tl, tlR = AR.alloc([3, 24], F32, "tl")
            tl2, tl2R = s2all.rearrange("p (k b) -> p k b", k=3), s2allR
            st0, st0R = AR.alloc([3, 24], F32, "st0")
            st1, st1R = AR.alloc([3, 24], F32, "st1")
            hs, _ = AR.alloc([D], F32, "hs"); hsRs = [Res(f"hs{i}") for i in range(4)]
            hb, _ = AR.alloc([D], BF16, "hb"); hbRs = [Res(f"hb{i}") for i in range(4)]
            m_g = AR.mark()
            raws = [AR.alloc([T + 3 * nseq], F32, f"raw{i}") for i in range(2)]
            acc, accR = AR.alloc([T], F32, "acc")
            xcs = [AR.alloc([T], BF16, f"xc{i}") for i in range(2)]
            if kind == "ctx":
                S.op("dve", lambda h: h.memset(tl, 0.0), writes=[tlR])
            else:
                S.dma("sp", lambda h: h.dma_start(out=tl, in_=tail_d), reads=[tail_dR], writes=[tlR])
            rc = [t0 + 3 * i for i, (t0, n) in enumerate(seqs)]

            def col(t):
                for i, (t0, n) in enumerate(seqs):
                    if t0 <= t <= t0 + n:
                        last = i
                        if t < t0 + n:
                            return rc[i] + 3 + (t - t0)
                return rc[last] + 3 + (t - seqs[last][0])

            def epi_x(ps, psR, c0, ns, t0, t1):
                raw, rawR = raws[(c0 // 128) % 2]
                a = col(t0)
                cp("act", raw[:, a:a + (t1 - t0)], ps, [psR], [rawR])

            def post_x(c0, ns):
                cbi = c0 // 128
                raw, rawR = raws[cbi % 2]
                for i, (t0, n) in enumerate(seqs):
                    hsrc, hsrcR = (tl, tlR) if i == 0 else (tl2, tl2R)
                    r0 = rc[i]
                    cp("dve", raw[:, r0:r0 + 3], hsrc[:, :, cbi], [hsrcR], [rawR])
                    S.op("dve", lambda h, r0=r0, n=n, t0=t0: h.tensor_scalar(
                        out=acc[:, t0:t0 + n], in0=raw[:, r0:r0 + n], scalar1=cwall[:, cbi:cbi + 1], scalar2=cwall[:, 96 + cbi:97 + cbi],
                        op0=ALU.mult, op1=ALU.add), reads=[rawR, cwR, cbR], writes=[accR])
                    for k in range(1, 4):
                        S.op("dve", lambda h, r0=r0, n=n, t0=t0, k=k: h.scalar_tensor_tensor(
                            out=acc[:, t0:t0 + n], in0=raw[:, r0 + k:r0 + k + n], scalar=cwall[:, k * 24 + cbi:k * 24 + cbi + 1],
                            in1=acc[:, t0:t0 + n], op0=ALU.mult, op1=ALU.add), reads=[rawR, cwR, accR], writes=[accR])
                    sto, stoR = (st0, st0R) if i == 0 else (st1, st1R)
                    cp("dve", sto[:, :, cbi], raw[:, r0 + n:r0 + n + 3], [rawR], [stoR])
                dst, dstR = xdst(cbi)
                S.op("act", lambda h: h.activation(out=dst, in_=acc, func=AF.Silu), reads=[accR], writes=[dstR])

            def xdst(cbi):
                if cbi < 16:
                    return xcs[cbi % 2]
                elif cbi < 20:
                    return BT[:, cbi - 16, :], BTR
                return CT[:, cbi - 20, :], CTR

            def post_late_x(c0, ns):
                cbi = c0 // 128
                dst, dstR = xdst(cbi)
                if cbi < 20:
                    tm, tmR = (xTM, xTMR) if cbi < 16 else (BTM, BTMR)
                    cc0 = (cbi if cbi < 16 else cbi - 16) * 128
                    tls = tiles(0, T, 128)
                    for b0 in range(0, len(tls), 8):
                        grp = tls[b0:b0 + 8]
                        pb, pbR = B[6 + (b0 // 8) % 2]

                        def tr(h, grp=grp, pb=pb):
                            for j, (a0, a1) in enumerate(grp):
                                r = h.transpose(out=bf(pb)[0:a1 - a0, j * 128:(j + 1) * 128], in_=dst[:, a0:a1],
                                                identity=ident)
                            return r
                        S.op("pe", tr, reads=[dstR, identR], writes=[pbR])
                        full = [q for q in grp if q[1] - q[0] == 128]
                        if full:
                            nf = len(full)
                            cp("act", tm[:, b0:b0 + nf, cc0:cc0 + 128],
                               bf(pb).rearrange("p (a b) -> p a b", a=8)[:, 0:nf, :], [pbR], [tmR])
                        for j, (a0, a1) in enumerate(grp):
                            if a1 - a0 < 128:
                                cp("act", tm[0:a1 - a0, b0 + j, cc0:cc0 + 128], bf(pb)[0:a1 - a0, j * 128:(j + 1) * 128],
                                   [pbR], [tmR])

            gemm(hT, hTR, 16, w_in, COL["b_xbc"], 3072, "FM", tiles(0, T, 512), epi_x, B[0:4], post=post_x, post_late=post_late_x)

            def epi_dt(ps, psR, c0, ncol, t0, t1):
                n = t1 - t0
                ti = t0 // 128
                S.op("dve", lambda h: h.tensor_tensor(out=dtv[0:n, ti, :], in0=ps, in1=dtb[0:n, :], op=ALU.add),
                     reads=[psR, dtbR], writes=[dtvR])
                S.op("act", lambda h: h.activation(out=dtv[0:n, ti, :], in_=dtv[0:n, ti, :], func=AF.Exp),
                     reads=[dtvR], writes=[dtvR])
                S.op("act", lambda h: h.activation(out=dtv[0:n, ti, :], in_=dtv[0:n, ti, :], func=AF.Ln, bias=1.0),
                     reads=[dtvR], writes=[dtvR])
                S.op("dve", lambda h: h.tensor_tensor(out=dta[0:n, ti, :], in0=dtv[0:n, ti, :], in1=arow[0:n, :],
                                                      op=ALU.mult), reads=[dtvR, arowR], writes=[dtaR])
            gemm(hT, hTR, 16, w_in, COL["b_dt"], 32, "TM", tiles(0, T, 128), epi_dt, B[0:4])

            if need_y:
                zsts = [AR.alloc([256], BF16, f"zst{i}") for i in range(2)]
                zi = [0]

                def epi_z(ps, psR, c0, ncol, t0, t1):
                    n = t1 - t0
                    zst, zstR = zsts[zi[0] % 2]
                    zi[0] += 1
                    S.op("act", lambda h: h.activation(out=zst[0:n, :], in_=ps, func=AF.Silu), reads=[psR], writes=[zstR])
                    S.dma("sp", lambda h: h.dma_start(out=zs_d[t0:t1, c0:c0 + ncol], in_=zst[0:n, 0:ncol]),
                          reads=[zstR], writes=[dR["zs_d"]])
                gemm(hT, hTR, 16, w_in, COL["b_z"], 2048, "TM", tiles(0, T, 128), epi_z, B[0:4])

            S.fence()
            AR.release(m_g)
            wk = [dict(rhsU=AR.alloc([8, 128], F32, f"rhsU{i}"), Eb=AR.alloc([8, 128], BF16, f"Eb{i}"),
                       wgt=AR.alloc([8, 128], BF16, f"wgt{i}"), cbm=AR.alloc([128], BF16, f"cbm{i}"),
                       dte=AR.alloc([8], F32, f"dte{i}"), xw=AR.alloc([512], BF16, f"xw{i}"),
                       xdt=AR.alloc([512], BF16, f"xdt{i}"), ytmp=AR.alloc([512], F32, f"ytmp{i}")) for i in range(2)]
            ecumA, ecumR = AR.alloc([ntile, 32], F32, "ecumA")
            dteA, dteAR = AR.alloc([ntile, 32], F32, "dteA")
            cdrA, cdrR = AR.alloc([ntile, 32], F32, "cdrA")
            ygs = [AR.alloc([4, 512], F32, f"yg{i}") for i in range(2)]
            zts = [AR.alloc([D], BF16, "zt")] * 2
            ynbs = [AR.alloc([D], BF16, f"ynb{i}") for i in range(2)]
            ssqs = [AR.alloc([8], F32, f"ssq{i}") for i in range(2)]
            ybsts = [AR.alloc([16, 128], BF16, "ybst")] * 2
            stg, stgR = ygs[0][0].rearrange("p a b -> p (a b)").rearrange("p (a b) -> p a b", a=16), ygs[0][1]

            def do_group(g, n, ti, t0, t1):
                gs = slice(g * 8, (g + 1) * 8)
                w_ = wk[g % 2]
                rhsU, rhsUR = w_["rhsU"]; Eb, EbR = w_["Eb"]; wgt, wgtR = w_["wgt"]; cbm, cbmR = w_["cbm"]
                dte, dteR = w_["dte"]; xw, xwR = w_["xw"]; xdt, xdtR = w_["xdt"]; ytmp, ytmpR = w_["ytmp"]
                yg, ygR = ygs[ti % 2]; zt, ztR = zts[ti % 2]; ssq, ssqR = ssqs[ti % 2]
                if need_y:
                    S.op("dve", lambda h, n=n, ti=ti, gs=gs, rhsU=rhsU: h.tensor_tensor(
                        out=rhsU[0:n, :, 0:n], in0=U[0:n, 0:n].unsqueeze(1).to_broadcast([n, 8, n]),
                        in1=dta[0:n, ti, gs].unsqueeze(2).to_broadcast([n, 8, n]), op=ALU.mult),
                        reads=[UR, dtaR], writes=[rhsUR])
                    for hh in range(2):
                        pb, pbR = B[2 * (g % 2) + hh]
                        yield
                        S.op("pe", lambda h, n=n, hh=hh, pb=pb: h.matmul(
                            pb[0:n, 0:4 * n].rearrange("p (a b) -> p a b", a=4), lhsT=LT[0:n, 0:n],
                            rhs=rhsU[0:n, hh * 4:(hh + 1) * 4, 0:n], start=True, stop=True),
                            reads=[LTR, rhsUR], writes=[pbR])
                        yield
                        S.op("act", lambda h, n=n, hh=hh, pb=pb: h.activation(
                            out=Eb[0:n, hh * 4:(hh + 1) * 4, 0:n],
                            in_=pb[0:n, 0:4 * n].rearrange("p (a b) -> p a b", a=4), func=AF.Exp),
                            reads=[pbR], writes=[EbR])

                    yield
                S.op("dve", lambda h, n=n, ti=ti, g=g: h.tensor_tensor(
                    out=xw[0:n, :].rearrange("p (a b) -> p a b", a=8),
                    in0=xTM[0:n, ti, g * 512:(g + 1) * 512].rearrange("p (a b) -> p a b", a=8),
                    in1=dteA[0:n, ti, gs].unsqueeze(2).to_broadcast([n, 8, 64]), op=ALU.mult),
                    reads=[xTMR, dteAR], writes=[xwR])
                if need_y:
                    pbo, pboR = B[7]
                    yield
                    S.op("pe", lambda h, n=n, t0=t0, t1=t1, g=g, pbo=pbo: h.matmul(
                        pbo[0:n, :], lhsT=CT[:, g, t0:t1], rhs=hb[:, g * 512:(g + 1) * 512], start=True, stop=True),
                        reads=[CTR, hbRs[g]], writes=[pboR])
                    S.op("dve", lambda h, n=n, gs=gs, pbo=pbo: h.tensor_tensor(
                        out=ytmp[0:n, :].rearrange("p (a b) -> p a b", a=8),
                        in0=pbo[0:n, :].rearrange("p (a b) -> p a b", a=8),
                        in1=ecumA[0:n, ti, gs].unsqueeze(2).to_broadcast([n, 8, 64]), op=ALU.mult),
                        reads=[pboR, ecumR], writes=[ytmpR])
                yield
                S.op("dve", lambda h, g=g, gs=gs: h.tensor_tensor(
                    out=hs[:, g * 512:(g + 1) * 512].rearrange("p (a b) -> p a b", a=8),
                    in0=hs[:, g * 512:(g + 1) * 512].rearrange("p (a b) -> p a b", a=8),
                    in1=cdrA[:, ti, gs].unsqueeze(2).to_broadcast([128, 8, 64]), op=ALU.mult),
                    reads=[hsRs[g], cdrR], writes=[hsRs[g]])
                pbs, pbsR = B[4]
                yield
                S.op("pe", lambda h, n=n, ti=ti, g=g, pbs=pbs: h.matmul(
                    pbs[:, :], lhsT=BTM[0:n, ti, g * 128:(g + 1) * 128], rhs=xw[0:n, :], start=True, stop=True),
                    reads=[BTMR, xwR], writes=[pbsR])
                S.op("dve", lambda h, g=g, pbs=pbs: h.tensor_tensor(
                    out=hs[:, g * 512:(g + 1) * 512], in0=hs[:, g * 512:(g + 1) * 512], in1=pbs[:, :], op=ALU.add),
                    reads=[hsRs[g], pbsR], writes=[hsRs[g]])
                yield
                cp("act", hb[:, g * 512:(g + 1) * 512], hs[:, g * 512:(g + 1) * 512], [hsRs[g]], [hbRs[g]])
                if need_y:
                    pbc, pbcR = B[5]
                    yield
                    S.op("pe", lambda h, n=n, t0=t0, t1=t1, g=g, pbc=pbc: h.matmul(
                        pbc[0:n, 0:n], lhsT=BT[:, g, t0:t1], rhs=CT[:, g, t0:t1], start=True, stop=True),
                        reads=[BTR, CTR], writes=[pbcR])
                    S.op("dve", lambda h, n=n, pbc=pbc: h.tensor_tensor(out=cbm[0:n, 0:n], in0=pbc[0:n, 0:n],
                                                                        in1=U[0:n, 0:n], op=ALU.mult),
                         reads=[pbcR, UR], writes=[cbmR])
                    yield
                    S.op("dve", lambda h, n=n: h.tensor_tensor(
                        out=wgt[0:n, :, 0:n], in0=Eb[0:n, :, 0:n],
                        in1=cbm[0:n, 0:n].unsqueeze(1).to_broadcast([n, 8, n]), op=ALU.mult),
                        reads=[EbR, cbmR], writes=[wgtR])
                    yield
                    S.op("dve", lambda h, n=n, ti=ti, g=g, gs=gs: h.tensor_tensor(
                        out=xdt[0:n, :].rearrange("p (a b) -> p a b", a=8),
                        in0=xTM[0:n, ti, g * 512:(g + 1) * 512].rearrange("p (a b) -> p a b", a=8),
                        in1=dtv[0:n, ti, gs].unsqueeze(2).to_broadcast([n, 8, 64]), op=ALU.mult),
                        reads=[xTMR, dtvR], writes=[xdtR])
                    pbd, pbdR = B[6]

                    def ydiag(h, n=n, ti=ti, g=g, pbd=pbd):
                        for hd in range(8):
                            h.matmul(pbd[0:n, hd * 64:(hd + 1) * 64], lhsT=wgt[0:n, hd, 0:n],
                                     rhs=xdt[0:n, hd * 64:(hd + 1) * 64], start=True, stop=False)
                            r = h.matmul(pbd[0:n, hd * 64:(hd + 1) * 64], lhsT=IDm[0:n, g * 8 + hd, 0:n],
                                         rhs=xTM[0:n, ti, g * 512 + hd * 64:g * 512 + (hd + 1) * 64],
                                         start=False, stop=True)
                        return r
                    yield
                    S.op("pe", ydiag, reads=[wgtR, xdtR, IDmR, xTMR], writes=[pbdR])
                    S.op("dve", lambda h, n=n, pbd=pbd: h.tensor_tensor(out=ytmp[0:n, :], in0=pbd[0:n, :],
                                                                        in1=ytmp[0:n, :], op=ALU.add),
                         reads=[pbdR, ytmpR], writes=[ytmpR])
                    yield
                    S.op("dve", lambda h, n=n, g=g: h.tensor_tensor(out=yg[0:n, g, :], in0=ytmp[0:n, :],
                                                                    in1=zt[0:n, g * 512:(g + 1) * 512], op=ALU.mult),
                         reads=[ytmpR, ztR], writes=[ygR])
                    yield
                    S.op("act", lambda h, n=n, g=g: h.activation(out=ytmp[0:n, :], in_=yg[0:n, g, :], func=AF.Square,
                                                                 scale=1.0 / math.sqrt(512.0),
                                                                 accum_out=ssq[0:n, g:g + 1]),
                         reads=[ygR], writes=[ytmpR, ssqR])


            for si, (s0, sn) in enumerate(seqs):
                if kind == "ctx":
                    S.op("dve", lambda h: h.memset(hs, 0.0), writes=hsRs)
                    S.op("dve", lambda h: h.memset(hb, 0.0), writes=hbRs)
                elif si == 0:
                    S.dma("sp", lambda h: h.dma_start(out=hs, in_=hs_d), reads=[hs_dR], writes=hsRs)
                    cp("act", hb, hs, hsRs, hbRs)
                else:
                    S.dma("sp", lambda h: h.dma_start(out=stg, in_=sssm.rearrange("(c p) n -> p c n", p=128)),
                          writes=[stgR])
                    for q in range(4):
                        pb, pbR = B[q % 2]

                        def tr(h, q=q, pb=pb):
                            for j in range(4):
                                r = h.transpose(out=pb[:, j * 128:(j + 1) * 128], in_=stg[:, q * 4 + j, :], identity=identf)
                            return r
                        S.op("pe", tr, reads=[stgR, identfR], writes=[pbR])
                        cp("act", hs[:, q * 512:(q + 1) * 512], pb[:, :], [pbR], [hsRs[q]])
                    cp("act", hb, hs, hsRs, hbRs)

                def tile_end(n, ti, t0, t1):
                    yg, ygR = ygs[ti % 2]; ssq, ssqR = ssqs[ti % 2]; ynb, ynbR = ynbs[ti % 2]; ybst, ybstR = ybsts[ti % 2]
                    S.op("act", lambda h: h.activation(out=ssq[0:n, 4:8], in_=ssq[0:n, 0:4], func=AF.Ln, bias=EPS),
                         reads=[ssqR], writes=[ssqR])
                    yield
                    S.op("act", lambda h: h.activation(out=ssq[0:n, 4:8], in_=ssq[0:n, 4:8], func=AF.Exp, scale=-0.5), reads=[ssqR], writes=[ssqR])
                    yield
                    S.op("dve", lambda h: h.tensor_tensor(
                        out=ynb[0:n, :].rearrange("p (a b) -> p a b", a=4), in0=yg[0:n, :, :],
                        in1=ssq[0:n, 4:8].unsqueeze(2).to_broadcast([n, 4, 512]), op=ALU.mult),
                        reads=[ygR, ssqR], writes=[ynbR])
                    for half in range(2):
                        pb, pbR = B[6 + half]
                        yield

                        def tr(h, half=half, pb=pb):
                            for j in range(8):
                                r = h.transpose(out=bf(pb)[:, j * 128:j * 128 + n],
                                                in_=ynb[0:n, (half * 8 + j) * 128:(half * 8 + j + 1) * 128],
                                                identity=ident[0:n, 0:n])
                            return r
                        S.op("pe", tr, reads=[ynbR, identR], writes=[pbR])
                        S.op("dve", lambda h, half=half, pb=pb: h.tensor_tensor(
                            out=ybst[:, half * 8:(half + 1) * 8, 0:n],
                            in0=bf(pb).rearrange("p (a b) -> p a b", a=8)[:, :, 0:n],
                            in1=g_ssm[:, half * 8:(half + 1) * 8].unsqueeze(2).to_broadcast([128, 8, n]), op=ALU.mult),
                            reads=[pbR, g_ssmR], writes=[ybstR])
                    yield
                    S.dma("sp", lambda h: h.dma_start(out=ybT_d[:, :, t0:t1], in_=ybst[:, :, 0:n]),
                          reads=[ybstR], writes=[ybT_dR])

                for (t0, t1) in tiles(s0, s0 + sn, 128):
                    n = t1 - t0
                    ti = t0 // 128
                    if need_y:
                        pb, pbR = B[4 + ti % 2]
                        S.op("pe", lambda h, n=n, ti=ti, pb=pb: h.matmul(pb[0:n, 0:32], lhsT=U[0:n, 0:n], rhs=dta[0:n, ti, :],
                                                                          start=True, stop=True),
                             reads=[UR, dtaR], writes=[pbR])
                        S.op("act", lambda h, n=n, ti=ti, pb=pb: h.activation(out=ecumA[0:n, ti, :], in_=pb[0:n, 0:32], func=AF.Exp),
                             reads=[pbR], writes=[ecumR])
                    pb, pbR = B[6 + ti % 2]
                    S.op("pe", lambda h, n=n, ti=ti, pb=pb: h.matmul(pb[:, 0:32], lhsT=onesf[0:n, :], rhs=dta[0:n, ti, :],
                                                                      start=True, stop=True),
                         reads=[onesfR, dtaR], writes=[pbR])
                    S.op("act", lambda h, ti=ti, pb=pb: h.activation(out=cdrA[:, ti, :], in_=pb[:, 0:32], func=AF.Exp),
                         reads=[pbR], writes=[cdrR])
                    pb, pbR = B[ti % 2]
                    S.op("pe", lambda h, n=n, ti=ti, pb=pb: h.matmul(pb[0:n, 0:32], lhsT=LT[0:n, 0:n], rhs=dta[0:n, ti, :],
                                                                      start=True, stop=True),
                         reads=[LTR, dtaR], writes=[pbR])
                    S.op("act", lambda h, n=n, ti=ti, pb=pb: h.activation(out=dteA[0:n, ti, :], in_=pb[0:n, 0:32], func=AF.Exp),
                         reads=[pbR], writes=[dteAR])
                    S.op("dve", lambda h, n=n, ti=ti: h.tensor_tensor(out=dteA[0:n, ti, :], in0=dteA[0:n, ti, :],
                                                                      in1=dtv[0:n, ti, :], op=ALU.mult),
                         reads=[dteAR, dtvR], writes=[dteAR])
                pend = None
                for (t0, t1) in tiles(s0, s0 + sn, 128):
                    n = t1 - t0
                    ti = t0 // 128
                    if need_y:
                        zt, ztR = zts[ti % 2]
                        S.dma("sp", lambda h, t0=t0, t1=t1, n=n, zt=zt: h.dma_start(out=zt[0:n, :], in_=zs_d[t0:t1, :]),
                              reads=[dR["zs_d"]], writes=[ztR])
                    for pair in ((0, 1), (2, 3)):
                        gens = [do_group(g, n, ti, t0, t1) for g in pair]
                        for _ in range(3):
                            next(gens[0])
                        if pend is not None:
                            gens.append(pend)
                            pend = None
                        while gens:
                            for gg in list(gens):
                                try:
                                    next(gg)
                                except StopIteration:
                                    gens.remove(gg)
                    if need_y:
                        pend = tile_end(n, ti, t0, t1)
                if pend is not None:
                    for _ in pend:
                        pass
                if kind == "ctx":
                    S.op("dve", lambda h: h.tensor_scalar(out=hs, in0=hs, scalar1=flg[:, 0:1], scalar2=None, op0=ALU.mult),
                         reads=hsRs + [flgR], writes=hsRs)
                    S.dma("sp", lambda h: h.dma_start(out=hs_d, in_=hs), reads=hsRs, writes=[hs_dR])
                    S.dma("sp", lambda h: h.dma_start(out=tail_d, in_=st0), reads=[st0R], writes=[tail_dR])
                else:
                    odst, odR = (ssm_p, dR["ssm_p"]) if si == 0 else (ssm_s, dR["ssm_s"])
                    for q in range(4):
                        pb, pbR = B[q % 2]

                        def tr(h, q=q, pb=pb):
                            for j in range(4):
                                r = h.transpose(out=pb[:, j * 128:(j + 1) * 128], in_=hs[:, (q * 4 + j) * 128:(q * 4 + j + 1) * 128],
                                                identity=identf)
                            return r
                        S.op("pe", tr, reads=[hsRs[q], identfR], writes=[pbR])
                        cp("act", stg[:, q * 4:(q + 1) * 4, :], pb[:, :].rearrange("p (a b) -> p a b", a=4), [pbR], [stgR])
                    S.dma("sp", lambda h, odst=odst: h.dma_start(out=odst.rearrange("(c p) n -> p c n", p=128), in_=stg),
                          reads=[stgR], writes=[odR])
                    cdst, cdR_ = (conv_p, dR["conv_p"]) if si == 0 else (conv_s, dR["conv_s"])
                    sto, stoR = (st0, st0R) if si == 0 else (st1, st1R)
                    pb, pbR = B[2]
                    S.op("pe", lambda h, sto=sto, pb=pb: h.transpose(out=pb[0:72, 0:128], in_=sto.rearrange("p k b -> p (k b)"),
                                                                     identity=identf), reads=[stoR, identfR], writes=[pbR])
                    cp("act", stg[0:72, 0, :], pb[0:72, 0:128], [pbR], [stgR])
                    S.dma("sp", lambda h, cdst=cdst: h.dma_start(out=cdst.rearrange("k (b p) -> (k b) p", p=128), in_=stg[0:72, 0, :]),
                          reads=[stgR], writes=[cdR_])
            AR.release(m)
            S.fence()
        def kv_proj(T, toff, kTb, kTbR, vSb, vSbR, kiTb, kiTbR, koff, outs):
            def epi_kv(ps, psR, c0, ncol, t0, t1):
                n = t1 - t0
                if outs is None:
                    if c0 == 256:
                        cp("act", vSb[0:n, (koff + t0) // 128, :], ps, [psR], [vSbR])
                else:
                    stf, stfR = cx["kvst"][kvi[0] % 2]
                    kvi[0] += 1
                    cp("dve", stf[0:n, 0:ncol], ps, [psR], [stfR])
                    if c0 == 256:
                        cp("act", vSb[0:n, (koff + t0) // 128, :], stf[0:n, 0:ncol], [stfR], [vSbR])
                    for (lo, hi, od, odR) in outs:
                        if lo <= t0 < hi:
                            dd = od[0] if c0 == 0 else od[1]
                            S.dma("sp", lambda h, dd=dd, t0=t0, lo=lo, n=n, stf=stf, ncol=ncol: h.dma_start(
                                out=dd[t0 - lo:t0 - lo + n, :], in_=stf[0:n, 0:ncol]), reads=[stfR], writes=[odR])
            gemm(hT, hTR, 16, w_in, COL["a_k"], 512, "TM", tiles(0, T, 128), epi_kv, B[0:4])

            def epi_kT(ps, psR, c0, ns, t0, t1):
                cp("act", kTb[:, c0 // 128, koff + t0:koff + t1], ps, [psR], [kTbR])
            gemm(hT, hTR, 16, w_in, COL["a_k"], 256, "FM", tiles(0, T, 512), epi_kT, B[0:4])

            def epi_kiT(ps, psR, c0, ns, t0, t1):
                cp("act", kiTb[:, koff + t0:koff + t1], ps, [psR], [kiTbR])
            gemm(hT, hTR, 16, w_in, COL["i_k"], 64, "FM", tiles(0, T, 512), epi_kiT, B[0:4], dupcols=2)
            if outs is not None:
                def epi_ki(ps, psR, c0, ncol, t0, t1):
                    n = t1 - t0
                    stf, stfR = cx["kvst"][kvi[0] % 2]
                    kvi[0] += 1
                    cp("dve", stf[0:n, 0:80], ps[:, 0:80], [psR], [stfR])
                    cp("act", cx["iw"][0:n, t0 // 128, :], stf[0:n, 64:80], [stfR], [cx["iwR"]])
                    for (lo, hi, od, odR) in outs:
                        if lo <= t0 < hi:
                            S.dma("sp", lambda h, od=od, t0=t0, lo=lo, n=n, stf=stf: h.dma_start(
                                out=od[2][t0 - lo:t0 - lo + n, :], in_=stf[0:n, 0:64]), reads=[stfR], writes=[odR])
                gemm(hT, hTR, 16, w_in, COL["i_k"], 80, "TM", tiles(0, T, 128), epi_ki, B[0:4])

        kvi = [0]
        cx = {}

        def ctx_kv_phase():
            global_m = AR.mark()
            kTc, kTcR = AR.alloc([2, TC], BF16, "kTc")
            vSc, vScR = AR.alloc([8, 256], BF16, "vSc")
            kiTc, kiTcR = AR.alloc([TC], BF16, "kiTc")
            kv_proj(TC, 0, kTc, kTcR, vSc, vScR, kiTc, kiTcR, 0, None)
            S.dma("sp", lambda h: h.dma_start(out=kT_d, in_=kTc), reads=[kTcR], writes=[kT_dR])
            S.dma("sp", lambda h: h.dma_start(out=vS_d, in_=vSc), reads=[vScR], writes=[vS_dR])
            S.dma("sp", lambda h: h.dma_start(out=kiT_d, in_=kiTc), reads=[kiTcR], writes=[kiT_dR])
            AR.release(global_m)
            S.fence()

        def attention_phase():
            m = AR.mark()
            L = TC + TO
            Ls = 1024 + TS
            kT, kTR = AR.alloc([2, L + TS], BF16, "kT")
            vS, vSR = AR.alloc([17, 256], BF16, "vS")
            kiT, kiTR = AR.alloc([L + TS], BF16, "kiT")
            kTs, kTsR = AR.alloc([2, 1024], BF16, "kTs")
            vSs, vSsR = AR.alloc([8, 256], BF16, "vSs")
            kiTs, kiTsR = AR.alloc([1024], BF16, "kiTs")
            qT, qTR = AR.alloc([8, TT], BF16, "qT")
            iqT, iqTR = AR.alloc([8, TT], BF16, "iqT")
            yaT, yaTR = AR.alloc([8, TT], BF16, "yaT")
            iw, iwR = AR.alloc([9, 16], F32, "iw")
            cx["iw"], cx["iwR"] = iw, iwR
            cx["kvst"] = [AR.alloc([256], F32, f"kvst{i}") for i in range(2)]
            S.dma("sp", lambda h: h.dma_start(out=kT[:, :, 0:TC], in_=kT_d), reads=[kT_dR], writes=[kTR])
            S.dma("sp", lambda h: h.dma_start(out=vS[:, 0:8, :], in_=vS_d), reads=[vS_dR], writes=[vSR])
            S.dma("sp", lambda h: h.dma_start(out=kiT[:, 0:TC], in_=kiT_d), reads=[kiT_dR], writes=[kiTR])
            outs = [(0, TO, (k_own[:, :], v_own[:, :], ki_own), dR["k_own"]),
                    (TO, TT, (k_s[:, :], v_s[:, :], ki_s), dR["k_s"])]
            if ATT_PRO >= 1:
                kv_proj(TT, 0, kT, kTR, vS, vSR, kiT, kiTR, TC, outs)
            ldf, ldfR = AR.alloc([256], F32, "ldf")
            ldb, ldbR = AR.alloc([256], BF16, "ldb")
            ldi, ldiR = AR.alloc([128], F32, "ldi")
            ldib, ldibR = AR.alloc([128], BF16, "ldib")
            for t in range(8 if ATT_PRO >= 2 else 0):
                S.dma("sp", lambda h, t=t: h.dma_start(out=ldf, in_=cv[t * 128:(t + 1) * 128, :]), writes=[ldfR])
                cp("dve", vSs[:, t, :], ldf, [ldfR], [vSsR])
                S.dma("sp", lambda h, t=t: h.dma_start(out=ldf, in_=ck[t * 128:(t + 1) * 128, :]), reads=[ldfR], writes=[ldfR])
                cp("dve", ldb, ldf, [ldfR], [ldbR])
                S.dma("sp", lambda h, t=t: h.dma_start(out=ldi[:, 0:64], in_=cki[t * 128:(t + 1) * 128, :]), writes=[ldiR])
                S.dma("sp", lambda h, t=t: h.dma_start(out=ldi[:, 64:128], in_=cki[t * 128:(t + 1) * 128, :]), reads=[ldiR], writes=[ldiR])
                cp("dve", ldib, ldi, [ldiR], [ldibR])
                pb, pbR = B[t % 2]

                def tr(h, pb=pb):
                    h.transpose(out=bf(pb)[:, 0:128], in_=ldb[:, 0:128], identity=ident)
                    h.transpose(out=bf(pb)[:, 128:256], in_=ldb[:, 128:256], identity=ident)
                    return h.transpose(out=bf(pb)[:, 256:384], in_=ldib, identity=ident)
                S.op("pe", tr, reads=[ldbR, ldibR, identR], writes=[pbR])
                cp("act", kTs[:, :, t * 128:(t + 1) * 128], bf(pb)[:, 0:256].rearrange("p (a b) -> p a b", a=2), [pbR], [kTsR])
                cp("act", kiTs[:, t * 128:(t + 1) * 128], bf(pb)[:, 256:384], [pbR], [kiTsR])

            def epi_q(dst, dstR):
                def f(ps, psR, c0, ns, t0, t1):
                    cp("act", dst[:, c0 // 128, t0:t1], ps, [psR], [dstR])
                return f
            if ATT_PRO >= 3:
                gemm(hT, hTR, 16, w_in, COL["a_q"], 1024, "FM", tiles(0, TT, 512), epi_q(qT, qTR), B[0:4])
                gemm(hT, hTR, 16, w_in, COL["i_q"], 1024, "FM", tiles(0, TT, 512), epi_q(iqT, iqTR), B[0:4])

            scs = [AR.alloc([L], F32, f"sc{i}") for i in range(2)]
            rrs = [AR.alloc([512], F32, f"rr{i}") for i in range(3)]
            jm, jkR = AR.alloc([2 * L], BF16, "jm")
            mkR = Res("mk")
            jk32 = jm.bitcast(F32)
            mk = jm[:, L:2 * L]
            mkTs = [AR.alloc([17, 128], BF16, f"mkT{i}") for i in range(2)]
            ees = [AR.alloc([512], BF16, f"ee{i}") for i in range(3)]
            pTs = [AR.alloc([512], BF16, f"pT{i}") for i in range(3)]
            rden, rdenR = AR.alloc([512], F32, "rden")
            bss = [AR.alloc([8], F32, f"bs{i}") for i in range(2)]

            def q_tile(q0, nq, segs, bias_ctx, last_mask, stage):
                qi = q0 // 128
                sc, scR = scs[qi % 2]
                mkT, mkTR = mkTs[qi % 2]
                bs, bsR = bss[qi % 2]
                if stage >= 1:
                    Lk = sum(sg[7] for sg in segs)
                kcol = 0
                if stage >= 1:
                    segs_idx = []
                else:
                    segs_idx = segs
                kmap = []
                for sg in segs_idx:
                    for c in range(0, sg[7], 512):
                        n = min(512, sg[7] - c)
                        kmap.append((kcol, n, sg, sg[6] + c))
                        kcol += n
                if stage == 0:
                    Lk = kcol
                units = [(s0, n, sg, c0, hd) for (s0, n, sg, c0) in kmap for hd in range(16)]

                def idx_front(u):
                    s0, n, sg, c0, hd = units[u]
                    pb, pbR = B[u % 2]
                    rr, rrR = rrs[u % 3]
                    half = (hd % 2) * 64
                    S.op("pe", lambda h: h.matmul(
                        pb[0:nq, 0:n], lhsT=iqT[half:half + 64, hd // 2, q0:q0 + nq],
                        rhs=sg[2][half:half + 64, c0:c0 + n], start=True, stop=True),
                        reads=[iqTR, sg[3]], writes=[pbR])
                    S.op("act", lambda h: h.activation(out=rr[0:nq, 0:n], in_=pb[0:nq, 0:n], func=AF.Relu),
                         reads=[pbR], writes=[rrR])

                def idx_back(u):
                    s0, n, sg, c0, hd = units[u]
                    rr, rrR = rrs[u % 3]
                    if hd == 0:
                        S.op(IDX_ENG, lambda h: h.tensor_scalar(
                            out=sc[0:nq, s0:s0 + n], in0=rr[0:nq, 0:n], scalar1=iw[0:nq, qi, 0:1], scalar2=None,
                            op0=ALU.mult), reads=[rrR, iwR], writes=[scR])
                    else:
                        S.op(IDX_ENG, lambda h: h.scalar_tensor_tensor(
                            out=sc[0:nq, s0:s0 + n], in0=rr[0:nq, 0:n], scalar=iw[0:nq, qi, hd:hd + 1],
                            in1=sc[0:nq, s0:s0 + n], op0=ALU.mult, op1=ALU.add), reads=[rrR, iwR, scR], writes=[scR])
                LOOK = 2
                for u in range(min(LOOK, len(units))):
                    idx_front(u)
                for u in range(len(units)):
                    if u + LOOK < len(units):
                        idx_front(u + LOOK)
                    idx_back(u)
                    yield
                if stage == 0:
                    if bias_ctx:
                        S.op(IDX_ENG, lambda h: h.tensor_scalar(out=sc[0:nq, 0:TC], in0=sc[0:nq, 0:TC], scalar1=flg[0:nq, 1:2],
                                                                scalar2=None, op0=ALU.add), reads=[scR, flgR], writes=[scR])
                    if last_mask:
                        S.op(IDX_ENG, lambda h: h.tensor_tensor(out=sc[0:nq, Lk - 128:Lk], in0=sc[0:nq, Lk - 128:Lk],
                                                                in1=cmask[0:nq, :], op=ALU.add), reads=[scR, cmaskR], writes=[scR])
                    return
                if stage == 1:
                    S.op("dve", lambda h: h.memset(bs[0:nq, 1:2], BIS_LO + BIS_W / 2.0), writes=[bsR])
                    for it in range(NBIS):
                        step = BIS_W / (2.0 ** (it + 1))
                        S.op("dve", lambda h: h.tensor_scalar(out=jk32[0:nq, 0:Lk], in0=sc[0:nq, 0:Lk], scalar1=bs[0:nq, 1:2],
                                                              scalar2=0.0, op0=ALU.is_ge, op1=ALU.add,
                                                              accum_out=bs[0:nq, 2:3]), reads=[scR, bsR], writes=[jkR, mkR, bsR])
                        S.op("dve", lambda h, step=step: h.tensor_scalar(out=bs[0:nq, 3:4], in0=bs[0:nq, 2:3],
                                                                         scalar1=float(TOPK) - 0.5, scalar2=step,
                                                                         op0=ALU.is_ge, op1=ALU.mult), reads=[bsR], writes=[bsR])
                        nstep = step / 2.0 if it < NBIS - 1 else step
                        S.op("dve", lambda h, nstep=nstep: h.scalar_tensor_tensor(
                            out=bs[0:nq, 1:2], in0=bs[0:nq, 3:4], scalar=nstep, in1=bs[0:nq, 1:2], op0=ALU.subtract, op1=ALU.add),
                            reads=[bsR], writes=[bsR])
                        yield
                    S.op("dve", lambda h: h.tensor_scalar(out=mk[0:nq, 0:Lk], in0=sc[0:nq, 0:Lk], scalar1=bs[0:nq, 1:2],
                                                          scalar2=None, op0=ALU.is_ge), reads=[scR, bsR], writes=[mkR])
                nkt = (Lk + 127) // 128
                for b0 in (range(0, nkt, 8) if stage == 1 else []):
                    pb, pbR = B[2 + (b0 // 8) % 2]
                    cnt = min(8, nkt - b0)

                    def tr(h, b0=b0, cnt=cnt, pb=pb):
                        for j in range(cnt):
                            kk = (b0 + j) * 128
                            nk = min(128, Lk - kk)
                            r = h.transpose(out=bf(pb)[0:nk, j * 128:j * 128 + nq], in_=mk[0:nq, kk:kk + nk],
                                            identity=ident[0:nq, 0:nq])
                        return r
                    S.op("pe", tr, reads=[mkR, identR], writes=[pbR])
                    for j in range(cnt):
                        kk = (b0 + j) * 128
                        nk = min(128, Lk - kk)
                        cp("act", mkT[0:nk, b0 + j, 0:nq], bf(pb)[0:nk, j * 128:j * 128 + nq], [pbR], [mkTR])
                if stage == 1:
                    return
                for g in range(2):
                    po, poR = B[4]
                    pd, pdR = B[5]
                    kts = []
                    kc = 0
                    for sg in segs:
                        for c in range(0, sg[7], 128):
                            nk = min(128, sg[7] - c)
                            kts.append((kc // 128, nk, sg, sg[6] + c, sg[8] + c // 128))
                            kc += nk
                    def front(idx, g=g):
                        kt, nk, sg, c0, vt = kts[idx]
                        pb, pbR = B[6 + idx % 2]
                        ee, eeR = ees[idx % 3]

                        def smm(h):
                            for a in range(4):
                                r = h.matmul(pb[0:nk, a * nq:(a + 1) * nq], lhsT=sg[0][:, g, c0:c0 + nk],
                                             rhs=qT[:, g * 4 + a, q0:q0 + nq], start=True, stop=True)
                            return r
                        S.op("pe", smm, reads=[sg[1], qTR], writes=[pbR])
                        S.op("act", lambda h: h.activation(out=ee[0:nk, 0:4 * nq], in_=pb[0:nk, 0:4 * nq],
                                                           func=AF.Exp, scale=1.0 / math.sqrt(128.0)),
                             reads=[pbR], writes=[eeR])

                    def back(idx, g=g, po=po, pd=pd, poR=poR, pdR=pdR):
                        kt, nk, sg, c0, vt = kts[idx]
                        ee, eeR = ees[idx % 3]
                        pT, pTR = pTs[idx % 3]
                        S.op("dve", lambda h: h.tensor_tensor(
                            out=pT[0:nk, 0:4 * nq].rearrange("p (a b) -> p a b", a=4),
                            in0=ee[0:nk, 0:4 * nq].rearrange("p (a b) -> p a b", a=4),
                            in1=mkT[0:nk, kt, 0:nq].unsqueeze(1).to_broadcast([nk, 4, nq]), op=ALU.mult),
                            reads=[eeR, mkTR], writes=[pTR])
                        first = idx == 0
                        lastk = idx == len(kts) - 1
                        S.op("pe", lambda h: h.matmul(
                            po[:, 0:4 * nq], lhsT=sg[4][0:nk, vt, g * 128:(g + 1) * 128], rhs=pT[0:nk, 0:4 * nq],
                            start=first, stop=lastk), reads=[sg[5], pTR], writes=[poR])
                        S.op("pe", lambda h: h.matmul(
                            pd[:, 0:4 * nq], lhsT=onesb[0:nk, :], rhs=pT[0:nk, 0:4 * nq], start=first, stop=lastk),
                            reads=[onesbR, pTR], writes=[pdR])
                    front(0)
                    for idx in range(len(kts)):
                        if idx + 1 < len(kts):
                            front(idx + 1)
                        back(idx)
                        yield
                    S.op("act", lambda h, pd=pd: h.activation(out=rden[:, 0:4 * nq], in_=pd[:, 0:4 * nq], func=AF.Ln),
                         reads=[pdR], writes=[rdenR])
                    S.op("act", lambda h: h.activation(out=rden[:, 0:4 * nq], in_=rden[:, 0:4 * nq], func=AF.Exp, scale=-1.0),
                         reads=[rdenR], writes=[rdenR])
                    S.op("dve", lambda h, po=po, g=g: h.tensor_tensor(
                        out=yaT[:, g * 4:(g + 1) * 4, q0:q0 + nq], in0=po[:, 0:4 * nq].rearrange("p (a b) -> p a b", a=4),
                        in1=rden[:, 0:4 * nq].rearrange("p (a b) -> p a b", a=4), op=ALU.mult),
                        reads=[poR, rdenR], writes=[yaTR])

            jobs = []
            for i in range(8):
                segs = [(kT, kTR, kiT, kiTR, vS, vSR, 0, TC + (i + 1) * 128, 0)]
                jobs.append((i * 128, 128, segs, True, True))
            segs = [(kTs, kTsR, kiTs, kiTsR, vSs, vSsR, 0, 1024, 0),
                    (kT, kTR, kiT, kiTR, vS, vSR, TC + TO, TS, 16)]
            jobs.append((TO, TS, segs, False, False))
            def nyield(job, stage):
                Lk_ = sum(sg[7] for sg in job[2])
                if stage == 0:
                    return 16 * ((Lk_ + 511) // 512)
                if stage == 1:
                    return NBIS
                return 2 * ((Lk_ + 127) // 128)
            nj = len(jobs)
            for st_ in range(nj + 2):
                act = []
                for stage, ji in ((2, st_ - 2), (1, st_ - 1), (0, st_)):
                    if 0 <= ji < nj:
                        act.append([q_tile(*jobs[ji], stage), nyield(jobs[ji], stage)])
                rounds = 30
                quota = [(-(-a[1] // rounds)) for a in act]
                while act:
                    for k_ in range(len(act) - 1, -1, -1):
                        for _ in range(quota[k_]):
                            try:
                                next(act[k_][0])
                            except StopIteration:
                                act.pop(k_)
                                quota.pop(k_)
                                break

            gt, gtR = AR.alloc([512], BF16, "gt")

            def epi_az(ps, psR, c0, ns, t0, t1):
                S.op("act", lambda h: h.activation(out=gt[:, 0:t1 - t0], in_=ps, func=AF.Silu), reads=[psR], writes=[gtR])
                S.op("dve", lambda h: h.tensor_tensor(out=yaT[:, c0 // 128, t0:t1], in0=yaT[:, c0 // 128, t0:t1],
                                                      in1=gt[:, 0:t1 - t0], op=ALU.mult), reads=[gtR, yaTR], writes=[yaTR])
            if ATT_PRO >= 4:
                gemm(hT, hTR, 16, w_in, COL["a_z"], 1024, "FM", tiles(0, TT, 512), epi_az, B[0:4])
            S.dma("sp", lambda h: h.dma_start(out=yaT_d, in_=yaT), reads=[yaTR], writes=[yaT_dR])
            AR.release(m)
            S.fence()
        def mem_phase():
            m = AR.mark()
            memhT, memhTR = AR.alloc([16, NM], BF16, "memhT")
            mkTp, mkTpR = AR.alloc([8, NM], BF16, "mkTp")
            mvp, mvpR = AR.alloc([2, 1024], BF16, "mvp")
            mkTs, mkTsR = AR.alloc([8, NM], BF16, "mkTs")
            mvs, mvsR = AR.alloc([2, 1024], BF16, "mvs")
            mqT, mqTR = AR.alloc([8, TT], BF16, "mqT")
            ymT, ymTR = AR.alloc([8, TT], BF16, "ymT")
            norm_rows(memx, NM, g_mem, g_memR, memhT, memhTR, 0, B[6:8])
            stf2 = [AR.alloc([256], F32, f"mst{i}") for i in range(2)]
            ci = [0]

            def epi_mk(ps, psR, c0, ncol, t0, t1):
                stf, stfR = stf2[ci[0] % 2]
                ci[0] += 1
                cp("dve", stf[:, 0:ncol], ps, [psR], [stfR])
                S.dma("sp", lambda h: h.dma_start(out=memk_o[t0:t1, c0:c0 + ncol], in_=stf[:, 0:ncol]),
                      reads=[stfR], writes=[dR["memk_o"]])
            gemm(memhT, memhTR, 16, w_mem_kv, 0, 1024, "TM", tiles(0, NM, 128), epi_mk, B[0:4])

            def epi_mv(ps, psR, c0, ncol, t0, t1):
                stf, stfR = stf2[ci[0] % 2]
                ci[0] += 1
                cp("dve", stf[:, 0:ncol], ps, [psR], [stfR])
                cp("act", mvp[:, t0 // 128, c0:c0 + ncol], stf[:, 0:ncol], [stfR], [mvpR])
                S.dma("sp", lambda h: h.dma_start(out=memv_o[t0:t1, c0:c0 + ncol], in_=stf[:, 0:ncol]),
                      reads=[stfR], writes=[dR["memv_o"]])
            gemm(memhT, memhTR, 16, w_mem_kv, 1024, 1024, "TM", tiles(0, NM, 128), epi_mv, B[0:4])

            def epi_mkT(ps, psR, c0, ns, t0, t1):
                cp("act", mkTp[:, c0 // 128, t0:t1], ps, [psR], [mkTpR])
            gemm(memhT, memhTR, 16, w_mem_kv, 0, 1024, "FM", [(0, NM)], epi_mkT, B[0:4])
            ldf, ldfR = AR.alloc([1024], F32, "mldf")
            ldb, ldbR = AR.alloc([1024], BF16, "mldb")
            for t in range(2):
                S.dma("sp", lambda h, t=t: h.dma_start(out=ldf, in_=cmv[t * 128:(t + 1) * 128, :]), writes=[ldfR])
                cp("dve", mvs[:, t, :], ldf, [ldfR], [mvsR])
                S.dma("sp", lambda h, t=t: h.dma_start(out=ldf, in_=cmk[t * 128:(t + 1) * 128, :]), reads=[ldfR], writes=[ldfR])
                cp("dve", ldb, ldf, [ldfR], [ldbR])
                pb, pbR = B[t % 2]

                def tr(h, pb=pb):
                    for j in range(8):
                        r = h.transpose(out=bf(pb)[:, j * 128:(j + 1) * 128], in_=ldb[:, j * 128:(j + 1) * 128], identity=ident)
                    return r
                S.op("pe", tr, reads=[ldbR, identR], writes=[pbR])
                cp("act", mkTs[:, :, t * 128:(t + 1) * 128], bf(pb).rearrange("p (a b) -> p a b", a=8), [pbR], [mkTsR])

            def epi_mq(ps, psR, c0, ns, t0, t1):
                cp("act", mqT[:, c0 // 128, t0:t1], ps, [psR], [mqTR])
            gemm(hT, hTR, 16, w_in, COL["m_q"], 1024, "FM", tiles(0, TT, 512), epi_mq, B[0:4])
            ee, eeR = AR.alloc([2, 512], BF16, "mee")
            rden, rdenR = AR.alloc([512], F32, "mrden")
            for (t0, t1) in tiles(0, TT, 512):
                nq = t1 - t0
                mkT_, mkR_, mv_, mvR_ = (mkTp, mkTpR, mvp, mvpR) if t0 < TO else (mkTs, mkTsR, mvs, mvsR)
                for hd in range(4):
                    for mt in range(2):
                        pb, pbR = B[mt]

                        def smm(h, pb=pb, mt=mt, hd=hd, mkT_=mkT_, t0=t0, t1=t1, nq=nq):
                            for c in range(2):
                                r = h.matmul(pb[:, 0:nq], lhsT=mkT_[:, hd * 2 + c, mt * 128:(mt + 1) * 128],
                                             rhs=mqT[:, hd * 2 + c, t0:t1], start=(c == 0), stop=(c == 1))
                            return r
                        S.op("pe", smm, reads=[mkR_, mqTR], writes=[pbR])
                        S.op("act", lambda h, pb=pb, mt=mt, nq=nq: h.activation(out=ee[:, mt, 0:nq], in_=pb[:, 0:nq], func=AF.Exp,
                                                                         scale=1.0 / 16.0), reads=[pbR], writes=[eeR])
                    pd, pdR = B[2]

                    def dmm(h, pd=pd, nq=nq):
                        for mt in range(2):
                            r = h.matmul(pd[:, 0:nq], lhsT=onesb, rhs=ee[:, mt, 0:nq], start=(mt == 0), stop=(mt == 1))
                        return r
                    S.op("pe", dmm, reads=[onesbR, eeR], writes=[pdR])
                    S.op("act", lambda h, pd=pd, nq=nq: h.activation(out=rden[:, 0:nq], in_=pd[:, 0:nq], func=AF.Ln),
                         reads=[pdR], writes=[rdenR])
                    S.op("act", lambda h, nq=nq: h.activation(out=rden[:, 0:nq], in_=rden[:, 0:nq], func=AF.Exp, scale=-1.0),
                         reads=[rdenR], writes=[rdenR])
                    for c in range(2):
                        po, poR = B[4 + c]

                        def omm(h, po=po, c=c, hd=hd, mv_=mv_, nq=nq):
                            for mt in range(2):
                                r = h.matmul(po[:, 0:nq], lhsT=mv_[:, mt, hd * 256 + c * 128:hd * 256 + (c + 1) * 128],
                                             rhs=ee[:, mt, 0:nq], start=(mt == 0), stop=(mt == 1))
                            return r
                        S.op("pe", omm, reads=[mvR_, eeR], writes=[poR])
                        S.op("dve", lambda h, po=po, c=c, hd=hd, t0=t0, t1=t1, nq=nq: h.tensor_tensor(
                            out=ymT[:, hd * 2 + c, t0:t1], in0=po[:, 0:nq], in1=rden[:, 0:nq], op=ALU.mult),
                            reads=[poR, rdenR], writes=[ymTR])
            gt, gtR = AR.alloc([512], BF16, "mgt")

            def epi_mz(ps, psR, c0, ns, t0, t1):
                S.op("act", lambda h: h.activation(out=gt[:, 0:t1 - t0], in_=ps, func=AF.Silu), reads=[psR], writes=[gtR])
                S.op("dve", lambda h: h.tensor_tensor(out=ymT[:, c0 // 128, t0:t1], in0=ymT[:, c0 // 128, t0:t1],
                                                      in1=gt[:, 0:t1 - t0], op=ALU.mult), reads=[gtR, ymTR], writes=[ymTR])
            gemm(hT, hTR, 16, w_in, COL["m_z"], 1024, "FM", tiles(0, TT, 512), epi_mz, B[0:4])
            S.dma("sp", lambda h: h.dma_start(out=ymT_d, in_=ymT), reads=[ymTR], writes=[ymT_dR])
            AR.release(m)
            S.fence()

        def merge_out_phase():
            m = AR.mark()
            mg, mgR = AR.alloc([16, TT], BF16, "mg")
            m2 = AR.mark()
            ys = {}
            for nm, d_, dR_, nb in (("a", yaT_d, yaT_dR, 8), ("b", ybT_d, ybT_dR, 16), ("m", ymT_d, ymT_dR, 8)):
                t_, tR_ = AR.alloc([nb, TT], BF16, "y" + nm)
                S.dma("sp", lambda h, t_=t_, d_=d_: h.dma_start(out=t_, in_=d_), reads=[dR_], writes=[tR_])
                ys[nm] = (t_, tR_, nb)
            tP, tPR = AR.alloc([2, TT], F32, "tP")
            tG, tGR = AR.alloc([512], F32, "tG")
            for bi_, (nm, W) in enumerate((("a", w_pa), ("b", w_pb), ("m", w_pm))):
                yT_, yR_, nb = ys[nm]
                for c0 in range(0, D, 256):
                    def epi_p(ps, psR, cc, ns, t0, t1):
                        cp("act", tP[:, cc // 128, t0:t1], ps, [psR], [tPR])
                    gemm(yT_, yR_, nb, W, c0, 256, "FM", tiles(0, TT, 512), epi_p, B[0:4])

                    def epi_g(ps, psR, cc, ns, t0, t1, c0=c0, bi_=bi_):
                        blk = (c0 + cc) // 128
                        n = t1 - t0
                        S.op("act", lambda h: h.activation(out=tG[:, 0:n], in_=ps, func=AF.Sigmoid), reads=[psR], writes=[tGR])
                        if bi_ == 0:
                            S.op("dve", lambda h: h.tensor_tensor(out=mg[:, blk, t0:t1], in0=tG[:, 0:n], in1=tP[:, cc // 128, t0:t1],
                                                                  op=ALU.mult), reads=[tGR, tPR], writes=[mgR])
                        else:
                            S.op("dve", lambda h: h.tensor_tensor(out=tG[:, 0:n], in0=tG[:, 0:n], in1=tP[:, cc // 128, t0:t1],
                                                                  op=ALU.mult), reads=[tGR, tPR], writes=[tGR])
                            S.op("dve", lambda h: h.tensor_tensor(out=mg[:, blk, t0:t1], in0=mg[:, blk, t0:t1], in1=tG[:, 0:n],
                                                                  op=ALU.add), reads=[tGR, mgR], writes=[mgR])
                    gemm(hT, hTR, 16, w_in, COL["gates"] + bi_ * D + c0, 256, "FM", tiles(0, TT, 512), epi_g, B[4:8])
            AR.release(m2)
            S.fence()
            res, resR = AR.alloc([9, D], F32, "res")
            gf, gfR = AR.alloc([D], F32, "gf")
            S.dma("sp", lambda h: h.dma_start(out=gf, in_=norm_final.partition_broadcast(128)), writes=[gfR])
            for ti in range(8):
                S.dma("sp", lambda h, ti=ti: h.dma_start(out=res[:, ti, :], in_=x_own[ti * 128:(ti + 1) * 128, :]), writes=[resR])
            S.dma("sp", lambda h: h.dma_start(out=res[0:TS, 8, :], in_=x_smp[:, :]), reads=[resR], writes=[resR])

            def epi_o(ps, psR, c0, ncol, t0, t1):
                n = t1 - t0
                S.op("dve", lambda h: h.tensor_tensor(out=res[0:n, t0 // 128, c0:c0 + ncol], in0=res[0:n, t0 // 128, c0:c0 + ncol],
                                                      in1=ps, op=ALU.add), reads=[psR, resR], writes=[resR])
            gemm(mg, mgR, 16, w_o, 0, D, "TM", tiles(0, TT, 128), epi_o, B[0:4])
            jk, jkR = AR.alloc([D], BF16, "ojk")
            for ti, (t0, t1) in enumerate(tiles(0, TT, 128)):
                n = t1 - t0
                S.op("act", lambda h, n=n, ti=ti: h.activation(out=jk[0:n, :], in_=res[0:n, ti, :], func=AF.Square,
                                                               scale=1.0 / math.sqrt(D), accum_out=sm[0:n, 4:5]),
                     reads=[resR], writes=[jkR, smR])
                S.op("act", lambda h, n=n: h.activation(out=sm[0:n, 5:6], in_=sm[0:n, 4:5], func=AF.Ln, bias=EPS),
                     reads=[smR], writes=[smR])
                S.op("act", lambda h, n=n: h.activation(out=sm[0:n, 6:7], in_=sm[0:n, 5:6], func=AF.Exp, scale=-0.5), reads=[smR], writes=[smR])
                S.op("dve", lambda h, n=n, ti=ti: h.scalar_tensor_tensor(out=res[0:n, ti, :], in0=res[0:n, ti, :], scalar=sm[0:n, 6:7],
                                                                         in1=gf[0:n, :], op0=ALU.mult, op1=ALU.mult),
                     reads=[resR, smR, gfR], writes=[resR])
                if t0 < TO:
                    S.dma("sp", lambda h, n=n, ti=ti, t0=t0: h.dma_start(out=y_own[t0:t0 + n, :], in_=res[0:n, ti, :]),
                          reads=[resR], writes=[dR["y_own"]])
                else:
                    S.dma("sp", lambda h, n=n, ti=ti: h.dma_start(out=y_smp[:, :], in_=res[0:n, ti, :]),
                          reads=[resR], writes=[dR["y_smp"]])
            AR.release(m)

        steps = [
            lambda: norm_rows(x_ctx, TC, g_in, g_inR, hT, hTR, 0, B[6:8]),
            ctx_kv_phase,
            lambda: ssm_pass(TC, [(0, TC)], False, "ctx"),
            lambda: (norm_rows(x_own, TO, g_in, g_inR, hT, hTR, 0, B[6:8]),
                     norm_rows(x_smp, TS, g_in, g_inR, hT, hTR, TO, B[6:8])),
            lambda: ssm_pass(TT, [(0, TO), (TO, TS)], True, "own"),
            attention_phase,
            mem_phase,
            merge_out_phase,
        ]
        for st_ in steps[:PHASE_LIMIT]:
            st_()
        S.emit(st)
    return nc


_NC = None


def kernel(**inp):
    global _NC
    f32 = lambda a: np.ascontiguousarray(np.asarray(a, dtype=np.float32))
    if _NC is None:
        _NC = build_program()
    nc = _NC
    xp = f32(inp["x_prompt"]); xs = f32(inp["x_sample"]); mp = f32(inp["mem_prompt"])
    shared = {
        "norm_in": f32(inp["norm_in"][0]), "w_in": f32(inp["w_in"][0]), "conv_w": f32(inp["conv_w"][0]),
        "conv_b": f32(inp["conv_b"][0]), "dt_bias": f32(inp["dt_bias"][0]), "a_log": f32(inp["a_log"][0]),
        "d_skip": f32(inp["d_skip"][0]), "ssm_norm": f32(inp["ssm_norm"][0]), "norm_mem": f32(inp["norm_mem"][0]),
        "w_mem_kv": f32(inp["w_mem_kv"][0]), "w_pa": f32(inp["w_pa"][0]), "w_pb": f32(inp["w_pb"][0]),
        "w_pm": f32(inp["w_pm"][0]), "w_o": f32(inp["w_o"][0]), "norm_final": f32(inp["norm_final"]),
    }
    in_maps = []
    for c in range(8):
        b, half = c // 2, c % 2
        fl = np.zeros((128, 2), np.float32)
        if half == 1:
            fl[:, 0] = 1.0
            xc = xp[b, 0:TC]
        else:
            fl[:, 1] = NEG
            xc = np.zeros((TC, D), np.float32)
        m = dict(shared)
        m.update({
            "x_own": f32(xp[b, half * TO:(half + 1) * TO]), "x_ctx": f32(xc), "x_smp": f32(xs[c]), "memx": f32(mp[b]),
            "flags": fl,
            "ck": f32(inp["cache_attn_k"][0, c].reshape(1024, 256)), "cv": f32(inp["cache_attn_v"][0, c].reshape(1024, 256)),
            "cki": f32(inp["cache_idx_k"][0, c]), "sconv": f32(inp["state_conv"][0, c]),
            "sssm": f32(inp["state_ssm"][0, c].reshape(2048, 128)),
            "cmk": f32(inp["cache_mem_k"][0, c].reshape(NM, 1024)), "cmv": f32(inp["cache_mem_v"][0, c].reshape(NM, 1024)),
        })
        in_maps.append(m)
    res = run_bass_kernel_spmd(nc, in_maps, core_ids=list(range(8)))
    R = res.results
    if DEBUG:
        kernel.dbg = R
    cat2 = lambda key, shp: np.stack([np.concatenate([R[2 * b][key], R[2 * b + 1][key]], axis=0) for b in range(4)]).reshape(shp)
    y_prompt = cat2("y_own", (4, 2048, D))
    y_sample = np.stack([R[c]["y_smp"] for c in range(8)])
    attn_k_p = cat2("k_own", (1, 4, 2048, 2, 128))
    attn_v_p = cat2("v_own", (1, 4, 2048, 2, 128))
    idx_k_p = cat2("ki_own", (1, 4, 2048, 64))
    conv_p = np.stack([R[2 * b + 1]["conv_p"] for b in range(4)]).reshape(1, 4, 3, 3072)
    ssm_p = np.stack([R[2 * b + 1]["ssm_p"] for b in range(4)]).reshape(1, 4, 32, 64, 128)
    mem_k_p = np.stack([R[2 * b]["memk_o"] for b in range(4)]).reshape(1, 4, NM, 4, 256)
    mem_v_p = np.stack([R[2 * b]["memv_o"] for b in range(4)]).reshape(1, 4, NM, 4, 256)
    attn_k_s = np.stack([R[c]["k_s"] for c in range(8)]).reshape(1, 8, TS, 2, 128)
    attn_v_s = np.stack([R[c]["v_s"] for c in range(8)]).reshape(1, 8, TS, 2, 128)
    idx_k_s = np.stack([R[c]["ki_s"] for c in range(8)]).reshape(1, 8, TS, 64)
    conv_s = np.stack([R[c]["conv_s"] for c in range(8)]).reshape(1, 8, 3, 3072)
    ssm_s = np.stack([R[c]["ssm_s"] for c in range(8)]).reshape(1, 8, 32, 64, 128)
    outs = (y_prompt, y_sample, attn_k_p, attn_v_p, idx_k_p, conv_p, ssm_p, mem_k_p, mem_v_p,
            attn_k_s, attn_v_s, idx_k_s, conv_s, ssm_s)
    return tuple(np.ascontiguousarray(o, dtype=np.float32) for o in outs)
```

```python
import math
from contextlib import ExitStack

import numpy as np
import concourse.bass as bass
import concourse.mybir as mybir
from concourse.bass_utils import run_bass_kernel_spmd

F32 = mybir.dt.float32
BF16 = mybir.dt.bfloat16
ALU = mybir.AluOpType
AF = mybir.ActivationFunctionType

D = 2048
TO = 1024
TS = 32
TT = TO + TS
TC = 1024
NM = 256
EPS = 1e-6
NEG = -30000.0
TOPK = 256
COL = dict(a_q=0, a_k=1024, a_v=1280, i_q=1536, i_k=2560, i_w=2624, a_z=2640, b_z=3664,
           b_xbc=5712, b_dt=8784, m_q=8816, m_z=9840, gates=10864)
NBIS = 27
DEBUG = False
PHASE_LIMIT = 99
ATT_STAGES = 4
ATT_PRO = 9
IDX_ENG = "dve"
BIS_LO = -512.0
BIS_W = 1024.0


class Res:
    __slots__ = ("name", "w", "rs")

    def __init__(self, name):
        self.name = name
        self.w = None
        self.rs = {}


class Sched:
    COMPUTE = ("pe", "act", "dve", "pool")
    ALL = ("pe", "act", "dve", "pool", "sp")

    def __init__(self, nc, ndma_sems=(("sp", 24), ("pool", 8))):
        self.nc = nc
        self.ops = {e: [] for e in self.ALL}
        self.cnt = {e: 0 for e in self.COMPUTE}
        self.seen = {e: {} for e in self.ALL}
        self.dma_slots = {q: [[f"dma_{q}_{i}", 0] for i in range(n)] for q, n in ndma_sems}
        self.dma_rr = {q: 0 for q, _ in ndma_sems}
        self.sems = {}
        self.fence_ev = {}

    def fence(self):
        f = {e: c for e, c in self.cnt.items() if c > 0}
        for q, slots in self.dma_slots.items():
            for s in slots:
                if s[1] > 0:
                    f[s[0]] = s[1]
        self.fence_ev = f

    def _deps(self, eng, reads, writes):
        deps = {}

        def add(k, v):
            if k == eng and eng == "pe":
                return
            if self.seen[eng].get(k, 0) >= v:
                return
            if deps.get(k, 0) < v:
                deps[k] = v
        if eng != "pool":
            for k, v in self.fence_ev.items():
                add(k, v)
        for r in reads:
            if r.w is not None:
                add(*r.w)
        for w in writes:
            if w.w is not None:
                add(*w.w)
            for k, v in w.rs.items():
                add(k, v)
        return deps

    def _commit(self, eng, deps, ev, reads, writes):
        for k, v in deps.items():
            self.seen[eng][k] = v
        k, v = ev
        for r in reads:
            if r.rs.get(k, 0) < v:
                r.rs[k] = v
        for w in writes:
            w.w = ev
            w.rs = {}

    def op(self, eng, fn, reads=(), writes=()):
        deps = self._deps(eng, reads, writes)
        self.cnt[eng] += 1
        ev = (eng, self.cnt[eng])
        self.ops[eng].append((list(deps.items()), fn, ev, 1))
        self._commit(eng, deps, ev, reads, writes)
        return ev

    def dma(self, q, fn, reads=(), writes=()):
        slots = self.dma_slots[q]
        i = self.dma_rr[q]
        self.dma_rr[q] = (i + 1) % len(slots)
        slot = slots[i]
        deps = self._deps(q, reads, writes)
        if slot[1] > 0 and self.seen[q].get(slot[0], 0) < slot[1]:
            if deps.get(slot[0], 0) < slot[1]:
                deps[slot[0]] = slot[1]
        slot[1] += 16
        ev = (slot[0], slot[1])
        self.ops[q].append((list(deps.items()), fn, ev, 16))
        self._commit(q, deps, ev, reads, writes)
        return ev

    def emit(self, stack):
        nc = self.nc
        names = list(self.COMPUTE)
        for q, slots in self.dma_slots.items():
            names += [s[0] for s in slots]
        for n in names:
            self.sems[n] = stack.enter_context(nc.semaphore(n))
        block = stack.enter_context(nc.Block())
        final = []
        for q, slots in self.dma_slots.items():
            for s in slots:
                if s[1] > 0:
                    final.append((s[0], s[1]))
        for e in self.COMPUTE:
            if self.cnt[e] > 0:
                final.append((e, self.cnt[e]))
        needed = {e: set() for e in self.COMPUTE}
        for e in self.ALL:
            for waits, fn, ev, inc in self.ops[e]:
                for k, v in waits:
                    if k in needed:
                        needed[k].add(v)
        for k, v in final:
            if k in needed:
                needed[k].add(v)
        remap = {}
        for e in self.COMPUTE:
            remap[e] = {v: i + 1 for i, v in enumerate(sorted(needed[e]))}

        def tr(k, v):
            return remap[k][v] if k in remap else v

        def replay(eng_name, extra_final=False):
            def body(h):
                for waits, fn, ev, inc in self.ops[eng_name]:
                    for k, v in waits:
                        h.wait_ge(self.sems[k], tr(k, v))
                    ins = fn(h)
                    if ev[0] not in remap or ev[1] in remap[ev[0]]:
                        ins.then_inc(self.sems[ev[0]], inc)
                if extra_final:
                    for k, v in final:
                        h.wait_ge(self.sems[k], tr(k, v))
            return body

        block.tensor(replay("pe"))
        block.scalar(replay("act"))
        block.vector(replay("dve"))
        block.gpsimd(replay("pool"))
        block.sync(replay("sp", extra_final=True))


class Arena:
    def __init__(self, nc, st, nbytes):
        self.t = st.enter_context(nc.sbuf_tensor("arena", [128, nbytes // 2], BF16))
        self.off = 0
        self.cap = nbytes
        self.k = 0

    def alloc(self, dims, dtype, name=None):
        esz = 4 if dtype == F32 else 2
        n = 1
        for d_ in dims:
            n *= d_
        nb = (n * esz + 63) // 64 * 64
        assert self.off + nb <= self.cap, ("arena overflow", name, self.off, nb, self.cap)
        v = self.t[:, self.off // 2:(self.off + n * esz) // 2]
        self.off += nb
        if dtype == F32:
            v = v.bitcast(F32)
        if len(dims) == 2:
            v = v.rearrange("p (a b) -> p a b", a=dims[0])
        elif len(dims) == 3:
            v = v.rearrange("p (a b c) -> p a b c", a=dims[0], b=dims[1])
        self.k += 1
        return v, Res(name or f"buf{self.k}")

    def mark(self):
        return self.off

    def release(self, m):
        self.off = m


def build_program():
    nc = bass.Bass("TRN2", target_bir_lowering=False)
    I = lambda n, s: nc.dram_tensor(n, list(s), F32, kind="ExternalInput").ap()
    O = lambda n, s: nc.dram_tensor(n, list(s), F32, kind="ExternalOutput").ap()
    x_own = I("x_own", (TO, D)); x_ctx = I("x_ctx", (TC, D)); x_smp = I("x_smp", (TS, D))
    memx = I("memx", (NM, D)); flags = I("flags", (128, 2))
    ck = I("ck", (1024, 256)); cv = I("cv", (1024, 256)); cki = I("cki", (1024, 64))
    sconv = I("sconv", (3, 3072)); sssm = I("sssm", (2048, 128))
    cmk = I("cmk", (NM, 1024)); cmv = I("cmv", (NM, 1024))
    norm_in = I("norm_in", (D,)); w_in = I("w_in", (D, 17008)); conv_w = I("conv_w", (4, 3072))
    conv_b = I("conv_b", (3072,)); dt_bias = I("dt_bias", (32,)); a_log = I("a_log", (32,))
    d_skip = I("d_skip", (32,)); ssm_norm = I("ssm_norm", (D,)); norm_mem = I("norm_mem", (D,))
    w_mem_kv = I("w_mem_kv", (D, 2048)); w_pa = I("w_pa", (1024, D)); w_pb = I("w_pb", (D, D))
    w_pm = I("w_pm", (1024, D)); w_o = I("w_o", (D, D)); norm_final = I("norm_final", (D,))

    y_own = O("y_own", (TO, D)); y_smp = O("y_smp", (TS, D))
    k_own = O("k_own", (TO, 256)); v_own = O("v_own", (TO, 256)); ki_own = O("ki_own", (TO, 64))
    conv_p = O("conv_p", (3, 3072)); ssm_p = O("ssm_p", (2048, 128))
    memk_o = O("memk_o", (NM, 1024)); memv_o = O("memv_o", (NM, 1024))
    k_s = O("k_s", (TS, 256)); v_s = O("v_s", (TS, 256)); ki_s = O("ki_s", (TS, 64))
    conv_s = O("conv_s", (3, 3072)); ssm_s = O("ssm_s", (2048, 128))
    zs_d = nc.dram_tensor("zs_d", [TT, D], BF16).ap()
    dR = {n: Res(n) for n in ["y_own", "y_smp", "k_own", "v_own", "ki_own", "conv_p", "ssm_p", "memk_o",
                              "memv_o", "k_s", "v_s", "ki_s", "conv_s", "ssm_s", "zs_d"]}

    with ExitStack() as st:
        S = Sched(nc)
        AR = Arena(nc, st, 212480)
        banks = []
        for i in range(8):
            b = st.enter_context(nc.psum_tensor(f"ps{i}", [128, 512], F32))
            banks.append((b, Res(f"ps{i}")))

        def bf(bank):
            return bank[:, :].bitcast(BF16)

        identf, identfR = AR.alloc([128], F32, "identf")
        ident, identR = AR.alloc([128], BF16, "ident")
        U, UR = AR.alloc([128], F32, "U")
        LT, LTR = AR.alloc([128], F32, "LT")
        onesf, onesfR = AR.alloc([128], F32, "onesf")
        onesb, onesbR = AR.alloc([128], BF16, "onesb")
        cmask, cmaskR = AR.alloc([128], F32, "cmask")
        IDm, IDmR = AR.alloc([32, 128], BF16, "IDm")
        gall, gallR = AR.alloc([48], F32, "gall")
        g_in, g_inR = gall[:, 0:16], gallR
        g_mem, g_memR = gall[:, 16:32], gallR
        g_ssm, g_ssmR = gall[:, 32:48], gallR
        cwall, cwallR = AR.alloc([120], F32, "cwall")
        cwR = cwallR
        cbR = cwallR
        s2all, s2allR = AR.alloc([72], F32, "s2all")
        vst, vstR = AR.alloc([128], F32, "vst")
        dtb, dtbR = AR.alloc([32], F32, "dtb")
        arow, arowR = AR.alloc([32], F32, "arow")
        dsk, dskR = AR.alloc([32], F32, "dsk")
        flg, flgR = AR.alloc([2], F32, "flg")
        sm, smR = AR.alloc([16], F32, "sm")

        S.op("pool", lambda h: h.memset(identf, 0.0), writes=[identfR])
        S.op("pool", lambda h: h.affine_select(out=identf, in_=identf, pattern=[[-1, 128]], compare_op=ALU.not_equal,
                                               fill=1.0, base=0, channel_multiplier=1), reads=[identfR], writes=[identfR])
        S.op("dve", lambda h: h.tensor_copy(out=ident, in_=identf), reads=[identfR], writes=[identR])
        S.op("pool", lambda h: h.memset(LT, 1.0), writes=[LTR])
        S.op("pool", lambda h: h.affine_select(out=LT, in_=LT, pattern=[[-1, 128]], compare_op=ALU.is_gt, fill=0.0,
                                               base=0, channel_multiplier=1), reads=[LTR], writes=[LTR])
        S.op("pool", lambda h: h.memset(U, 1.0), writes=[UR])
        S.op("pool", lambda h: h.affine_select(out=U, in_=U, pattern=[[1, 128]], compare_op=ALU.is_ge, fill=0.0,
                                               base=0, channel_multiplier=-1), reads=[UR], writes=[UR])
        S.op("pool", lambda h: h.memset(onesf, 1.0), writes=[onesfR])
        S.op("pool", lambda h: h.memset(onesb, 1.0), writes=[onesbR])
        S.op("pool", lambda h: h.memset(cmask, 0.0), writes=[cmaskR])
        S.op("pool", lambda h: h.memset(cmask[0:64, 64:128], NEG), reads=[cmaskR], writes=[cmaskR])
        def load_T(parts, dst, dstR):
            r = 0
            for src, n in parts:
                S.dma("sp", lambda h, src=src, r=r, n=n: h.dma_start(out=vst[r:r + n, :], in_=src), reads=[vstR], writes=[vstR])
                r += n
            pb, pbR = banks[7]
            S.op("pe", lambda h, r=r, pb=pb: h.transpose(out=pb[:, 0:r], in_=vst[0:r, :], identity=identf[0:r, 0:r]),
                 reads=[vstR, identfR], writes=[pbR])
            S.op("act", lambda h, r=r, pb=pb: h.activation(out=dst[:, 0:r], in_=pb[:, 0:r], func=AF.Copy),
                 reads=[pbR], writes=[dstR])
        load_T([(norm_in.rearrange("(c p) -> c p", p=128), 16), (norm_mem.rearrange("(c p) -> c p", p=128), 16),
                (ssm_norm.rearrange("(c p) -> c p", p=128), 16)], gall, gallR)
        load_T([(conv_w.rearrange("k (b p) -> (k b) p", p=128), 96), (conv_b.rearrange("(b p) -> b p", p=128), 24)],
               cwall, cwallR)
        load_T([(sconv.rearrange("k (b p) -> (k b) p", p=128), 72)], s2all, s2allR)
        S.dma("sp", lambda h: h.dma_start(out=dtb, in_=dt_bias.partition_broadcast(128)), writes=[dtbR])
        S.dma("sp", lambda h: h.dma_start(out=arow, in_=a_log.partition_broadcast(128)), writes=[arowR])
        S.dma("sp", lambda h: h.dma_start(out=dsk, in_=d_skip.partition_broadcast(128)), writes=[dskR])
        S.dma("sp", lambda h: h.dma_start(out=flg, in_=flags), writes=[flgR])
        S.op("act", lambda h: h.activation(out=arow, in_=arow, func=AF.Exp), reads=[arowR], writes=[arowR])
        S.op("dve", lambda h: h.tensor_scalar(out=arow, in0=arow, scalar1=-1.0, scalar2=None, op0=ALU.mult),
             reads=[arowR], writes=[arowR])
        S.op("dve", lambda h: h.tensor_tensor(out=IDm, in0=identf.unsqueeze(1).to_broadcast([128, 32, 128]),
                                              in1=dsk.unsqueeze(2).to_broadcast([128, 32, 128]), op=ALU.mult),
             reads=[identfR, dskR], writes=[IDmR])

        hT, hTR = AR.alloc([16, TT], BF16, "hT")
        wbufs = [AR.alloc([16, 256], BF16, f"wb{i}") for i in range(3)]
        wrr = [0]
        PH = AR.mark()

        def scratch(name, dims, dtype):
            if DEBUG and name in ("yaT_d", "ybT_d", "ymT_d"):
                return nc.dram_tensor(name, [128] + list(dims), dtype, kind="ExternalOutput").ap(), Res(name)
            return nc.dram_tensor(name, [128] + list(dims), dtype).ap(), Res(name)

        kT_d, kT_dR = scratch("kT_d", [2, TC], BF16)
        vS_d, vS_dR = scratch("vS_d", [8, 256], BF16)
        kiT_d, kiT_dR = scratch("kiT_d", [TC], BF16)
        hs_d, hs_dR = scratch("hs_d", [D], F32)
        tail_d, tail_dR = scratch("tail_d", [3, 24], F32)
        yaT_d, yaT_dR = scratch("yaT_d", [8, TT], BF16)
        ybT_d, ybT_dR = scratch("ybT_d", [16, TT], BF16)
        ymT_d, ymT_dR = scratch("ymT_d", [8, TT], BF16)

        def wnext():
            i = wrr[0]
            wrr[0] = (i + 1) % len(wbufs)
            return wbufs[i]

        def gemm(A, AR_, kc, W, wc0, wnc, mode, groups, epi, pbanks, post=None, dupcols=None, post_late=None):
            bi = 0
            late_q = []
            for c0 in range(wc0, wc0 + wnc, 256):
                ncol = min(256, wc0 + wnc - c0)
                wb, wbR = wnext()
                if dupcols is None:
                    S.dma("pool", lambda h, wb=wb, c0=c0, ncol=ncol: h.dma_start(
                        out=wb[:, 0:kc, 0:ncol], in_=W[:, c0:c0 + ncol].rearrange("(c p) n -> p c n", p=128)),
                        writes=[wbR])
                else:
                    for rep in range(dupcols):
                        S.dma("pool", lambda h, wb=wb, c0=c0, ncol=ncol, rep=rep: h.dma_start(
                            out=wb[:, 0:kc, rep * ncol:(rep + 1) * ncol],
                            in_=W[:, c0:c0 + ncol].rearrange("(c p) n -> p c n", p=128)), reads=[wbR], writes=[wbR])
                    ncol = ncol * dupcols
                if mode == "FM":
                    for s0 in range(0, ncol, 128):
                        ns = min(128, ncol - s0)
                        for (t0, t1) in groups:
                            pb, pbR = pbanks[bi % len(pbanks)]
                            bi += 1

                            def mm(h, wb=wb, s0=s0, ns=ns, t0=t0, t1=t1, pb=pb):
                                for k in range(kc):
                                    r = h.matmul(pb[0:ns, 0:t1 - t0], lhsT=wb[:, k, s0:s0 + ns], rhs=A[:, k, t0:t1],
                                                 start=(k == 0), stop=(k == kc - 1))
                                return r
                            S.op("pe", mm, reads=[wbR, AR_], writes=[pbR])
                            epi(pb[0:ns, 0:t1 - t0], pbR, c0 - wc0 + s0, ns, t0, t1)
                        if post is not None:
                            post(c0 - wc0 + s0, ns)
                        while late_q:
                            late_q.pop(0)()
                        if post_late is not None:
                            late_q.append(lambda a=c0 - wc0 + s0, b=ns: post_late(a, b))
                else:
                    for (t0, t1) in groups:
                        pb, pbR = pbanks[bi % len(pbanks)]
                        bi += 1

                        def mm(h, wb=wb, ncol=ncol, t0=t0, t1=t1, pb=pb):
                            for k in range(kc):
                                r = h.matmul(pb[0:t1 - t0, 0:ncol], lhsT=A[:, k, t0:t1], rhs=wb[:, k, 0:ncol],
                                             start=(k == 0), stop=(k == kc - 1))
                            return r
                        S.op("pe", mm, reads=[wbR, AR_], writes=[pbR])
                        epi(pb[0:t1 - t0, 0:ncol], pbR, c0 - wc0, ncol, t0, t1)
            while late_q:
                late_q.pop(0)()

        def tiles(t_lo, t_hi, step):
            return [(t, min(t + step, t_hi)) for t in range(t_lo, t_hi, step)]

        def norm_rows(src, nrows, gT, gTR, dstT, dstR, toff, tb):
            m = AR.mark()
            xts = [AR.alloc([D], F32, f"xt{i}") for i in range(2)]
            junk, junkR = AR.alloc([D], BF16, "junk")
            xn, xnR = AR.alloc([D], BF16, "xn")
            for i, r0 in enumerate(range(0, nrows, 128)):
                n = min(128, nrows - r0)
                xt, xtR = xts[i % 2]
                S.dma("sp", lambda h, xt=xt, r0=r0, n=n: h.dma_start(out=xt[0:n, :], in_=src[r0:r0 + n, :]),
                      writes=[xtR])
                S.op("act", lambda h, xt=xt, n=n: h.activation(out=junk[0:n, :], in_=xt[0:n, :], func=AF.Square,
                                                               scale=1.0 / math.sqrt(D), accum_out=sm[0:n, 0:1]),
                     reads=[xtR], writes=[junkR, smR])
                S.op("act", lambda h, n=n: h.activation(out=sm[0:n, 1:2], in_=sm[0:n, 0:1], func=AF.Ln, bias=EPS),
                     reads=[smR], writes=[smR])
                S.op("act", lambda h, n=n: h.activation(out=sm[0:n, 2:3], in_=sm[0:n, 1:2], func=AF.Exp, scale=-0.5), reads=[smR], writes=[smR])
                S.op("act", lambda h, xt=xt, n=n: h.activation(out=xn[0:n, :], in_=xt[0:n, :], func=AF.Copy,
                                                               scale=sm[0:n, 2:3]), reads=[xtR, smR], writes=[xnR])
                for half in range(2):
                    pb, pbR = tb[half % len(tb)]

                    def tr(h, n=n, half=half, pb=pb):
                        for j in range(8):
                            r = h.transpose(out=bf(pb)[:, j * 128:j * 128 + n],
                                            in_=xn[0:n, (half * 8 + j) * 128:(half * 8 + j + 1) * 128],
                                            identity=ident[0:n, 0:n])
                        return r
                    S.op("pe", tr, reads=[xnR, identR], writes=[pbR])
                    S.op("dve", lambda h, n=n, half=half, pb=pb, r0=r0: h.tensor_tensor(
                        out=dstT[:, half * 8:(half + 1) * 8, toff + r0:toff + r0 + n],
                        in0=bf(pb).rearrange("p (a b) -> p a b", a=8)[:, :, 0:n],
                        in1=gT[:, half * 8:(half + 1) * 8].unsqueeze(2).to_broadcast([128, 8, n]), op=ALU.mult),
                        reads=[pbR, gTR], writes=[dstR])
            AR.release(m)
            S.fence()

        B = banks

        def cp(eng, out, in_, reads, writes):
            if eng == "act":
                S.op("act", lambda h: h.activation(out=out, in_=in_, func=AF.Copy), reads=reads, writes=writes)
            else:
                S.op(eng, lambda h: h.tensor_copy(out=out, in_=in_), reads=reads, writes=writes)

        def ssm_pass(T, seqs, need_y, kind):
            m = AR.mark()
            ntile = (T + 127) // 128
            nseq = len(seqs)
            xTM, xTMR = AR.alloc([ntile, D], BF16, "xTM")
            BTM, BTMR = AR.alloc([ntile, 512], BF16, "BTM")
            BT, BTR = AR.alloc([4, T], BF16, "BT")
            CT, CTR = AR.alloc([4, T], BF16, "CT")
            dtv, dtvR = AR.alloc([ntile, 32], F32, "dtv")
            dta, dtaR = AR.alloc([ntile, 32], F32, "dta")
            tl, tlR = AR.alloc([3, 24], F32, "tl")
            tl2, tl2R = s2all.rearrange("p (k b) -> p k b", k=3), s2allR
            st0, st0R = AR.alloc([3, 24], F32, "st0")
            st1, st1R = AR.alloc([3, 24], F32, "st1")
            hs, _ = AR.alloc([D], F32, "hs"); hsRs = [Res(f"hs{i}") for i in range(4)]
            hb, _ = AR.alloc([D], BF16, "hb"); hbRs = [Res(f"hb{i}") for i in range(4)]
            m_g = AR.mark()
            raws = [AR.alloc([T + 3 * nseq], F32, f"raw{i}") for i in range(2)]
            acc, accR = AR.alloc([T], F32, "acc")
            xcs = [AR.alloc([T], BF16, f"xc{i}") for i in range(2)]
            if kind == "ctx":
                S.op("dve", lambda h: h.memset(tl, 0.0), writes=[tlR])
            else:
                S.dma("sp", lambda h: h.dma_start(out=tl, in_=tail_d), reads=[tail_dR], writes=[tlR])
            rc = [t0 + 3 * i for i, (t0, n) in enumerate(seqs)]

            def col(t):
                for i, (t0, n) in enumerate(seqs):
                    if t0 <= t <= t0 + n:
                        last = i
                        if t < t0 + n:
                            return rc[i] + 3 + (t - t0)
                return rc[last] + 3 + (t - seqs[last][0])

            def epi_x(ps, psR, c0, ns, t0, t1):
                raw, rawR = raws[(c0 // 128) % 2]
                a = col(t0)
                cp("act", raw[:, a:a + (t1 - t0)], ps, [psR], [rawR])

            def post_x(c0, ns):
                cbi = c0 // 128
                raw, rawR = raws[cbi % 2]
                for i, (t0, n) in enumerate(seqs):
                    hsrc, hsrcR = (tl, tlR) if i == 0 else (tl2, tl2R)
                    r0 = rc[i]
                    cp("dve", raw[:, r0:r0 + 3], hsrc[:, :, cbi], [hsrcR], [rawR])
                    S.op("dve", lambda h, r0=r0, n=n, t0=t0: h.tensor_scalar(
                        out=acc[:, t0:t0 + n], in0=raw[:, r0:r0 + n], scalar1=cwall[:, cbi:cbi + 1], scalar2=cwall[:, 96 + cbi:97 + cbi],
                        op0=ALU.mult, op1=ALU.add), reads=[rawR, cwR, cbR], writes=[accR])
                    for k in range(1, 4):
                        S.op("dve", lambda h, r0=r0, n=n, t0=t0, k=k: h.scalar_tensor_tensor(
                            out=acc[:, t0:t0 + n], in0=raw[:, r0 + k:r0 + k + n], scalar=cwall[:, k * 24 + cbi:k * 24 + cbi + 1],
                            in1=acc[:, t0:t0 + n], op0=ALU.mult, op1=ALU.add), reads=[rawR, cwR, accR], writes=[accR])
                    sto, stoR = (st0, st0R) if i == 0 else (st1, st1R)
                    cp("dve", sto[:, :, cbi], raw[:, r0 + n:r0 + n + 3], [rawR], [stoR])
                dst, dstR = xdst(cbi)
                S.op("act", lambda h: h.activation(out=dst, in_=acc, func=AF.Silu), reads=[accR], writes=[dstR])

            def xdst(cbi):
                if cbi < 16:
                    return xcs[cbi % 2]
                elif cbi < 20:
                    return BT[:, cbi - 16, :], BTR
                return CT[:, cbi - 20, :], CTR

            def post_late_x(c0, ns):
                cbi = c0 // 128
                dst, dstR = xdst(cbi)
                if cbi < 20:
                    tm, tmR = (xTM, xTMR) if cbi < 16 else (BTM, BTMR)
                    cc0 = (cbi if cbi < 16 else cbi - 16) * 128
                    tls = tiles(0, T, 128)
                    for b0 in range(0, len(tls), 8):
                        grp = tls[b0:b0 + 8]
                        pb, pbR = B[6 + (b0 // 8) % 2]

                        def tr(h, grp=grp, pb=pb):
                            for j, (a0, a1) in enumerate(grp):
                                r = h.transpose(out=bf(pb)[0:a1 - a0, j * 128:(j + 1) * 128], in_=dst[:, a0:a1],
                                                identity=ident)
                            return r
                        S.op("pe", tr, reads=[dstR, identR], writes=[pbR])
                        full = [q for q in grp if q[1] - q[0] == 128]
                        if full:
                            nf = len(full)
                            cp("act", tm[:, b0:b0 + nf, cc0:cc0 + 128],
                               bf(pb).rearrange("p (a b) -> p a b", a=8)[:, 0:nf, :], [pbR], [tmR])
                        for j, (a0, a1) in enumerate(grp):
                            if a1 - a0 < 128:
                                cp("act", tm[0:a1 - a0, b0 + j, cc0:cc0 + 128], bf(pb)[0:a1 - a0, j * 128:(j + 1) * 128],
                                   [pbR], [tmR])

            gemm(hT, hTR, 16, w_in, COL["b_xbc"], 3072, "FM", tiles(0, T, 512), epi_x, B[0:4], post=post_x, post_late=post_late_x)

            def epi_dt(ps, psR, c0, ncol, t0, t1):
                n = t1 - t0
                ti = t0 // 128
                S.op("dve", lambda h: h.tensor_tensor(out=dtv[0:n, ti, :], in0=ps, in1=dtb[0:n, :], op=ALU.add),
                     reads=[psR, dtbR], writes=[dtvR])
                S.op("act", lambda h: h.activation(out=dtv[0:n, ti, :], in_=dtv[0:n, ti, :], func=AF.Exp),
                     reads=[dtvR], writes=[dtvR])
                S.op("act", lambda h: h.activation(out=dtv[0:n, ti, :], in_=dtv[0:n, ti, :], func=AF.Ln, bias=1.0),
                     reads=[dtvR], writes=[dtvR])
                S.op("dve", lambda h: h.tensor_tensor(out=dta[0:n, ti, :], in0=dtv[0:n, ti, :], in1=arow[0:n, :],
                                                      op=ALU.mult), reads=[dtvR, arowR], writes=[dtaR])
            gemm(hT, hTR, 16, w_in, COL["b_dt"], 32, "TM", tiles(0, T, 128), epi_dt, B[0:4])

            if need_y:
                zsts = [AR.alloc([256], BF16, f"zst{i}") for i in range(2)]
                zi = [0]

                def epi_z(ps, psR, c0, ncol, t0, t1):
                    n = t1 - t0
                    zst, zstR = zsts[zi[0] % 2]
                    zi[0] += 1
                    S.op("act", lambda h: h.activation(out=zst[0:n, :], in_=ps, func=AF.Silu), reads=[psR], writes=[zstR])
                    S.dma("sp", lambda h: h.dma_start(out=zs_d[t0:t1, c0:c0 + ncol], in_=zst[0:n, 0:ncol]),
                          reads=[zstR], writes=[dR["zs_d"]])
                gemm(hT, hTR, 16, w_in, COL["b_z"], 2048, "TM", tiles(0, T, 128), epi_z, B[0:4])

            S.fence()
            AR.release(m_g)
            wk = [dict(rhsU=AR.alloc([8, 128], F32, f"rhsU{i}"), Eb=AR.alloc([8, 128], BF16, f"Eb{i}"),
                       wgt=AR.alloc([8, 128], BF16, f"wgt{i}"), cbm=AR.alloc([128], BF16, f"cbm{i}"),
                       dte=AR.alloc([8], F32, f"dte{i}"), xw=AR.alloc([512], BF16, f"xw{i}"),
                       xdt=AR.alloc([512], BF16, f"xdt{i}"), ytmp=AR.alloc([512], F32, f"ytmp{i}")) for i in range(2)]
            ecumA, ecumR = AR.alloc([ntile, 32], F32, "ecumA")
            dteA, dteAR = AR.alloc([ntile, 32], F32, "dteA")
            cdrA, cdrR = AR.alloc([ntile, 32], F32, "cdrA")
            ygs = [AR.alloc([4, 512], F32, f"yg{i}") for i in range(2)]
            zts = [AR.alloc([D], BF16, "zt")] * 2
            ynbs = [AR.alloc([D], BF16, f"ynb{i}") for i in range(2)]
            ssqs = [AR.alloc([8], F32, f"ssq{i}") for i in range(2)]
            ybsts = [AR.alloc([16, 128], BF16, "ybst")] * 2
            stg, stgR = ygs[0][0].rearrange("p a b -> p (a b)").rearrange("p (a b) -> p a b", a=16), ygs[0][1]

            def do_group(g, n, ti, t0, t1):
                gs = slice(g * 8, (g + 1) * 8)
                w_ = wk[g % 2]
                rhsU, rhsUR = w_["rhsU"]; Eb, EbR = w_["Eb"]; wgt, wgtR = w_["wgt"]; cbm, cbmR = w_["cbm"]
                dte, dteR = w_["dte"]; xw, xwR = w_["xw"]; xdt, xdtR = w_["xdt"]; ytmp, ytmpR = w_["ytmp"]
                yg, ygR = ygs[ti % 2]; zt, ztR = zts[ti % 2]; ssq, ssqR = ssqs[ti % 2]
                if need_y:
                    S.op("dve", lambda h, n=n, ti=ti, gs=gs, rhsU=rhsU: h.tensor_tensor(
                        out=rhsU[0:n, :, 0:n], in0=U[0:n, 0:n].unsqueeze(1).to_broadcast([n, 8, n]),
                        in1=dta[0:n, ti, gs].unsqueeze(2).to_broadcast([n, 8, n]), op=ALU.mult),
                        reads=[UR, dtaR], writes=[rhsUR])
                    for hh in range(2):
                        pb, pbR = B[2 * (g % 2) + hh]
                        yield
                        S.op("pe", lambda h, n=n, hh=hh, pb=pb: h.matmul(
                            pb[0:n, 0:4 * n].rearrange("p (a b) -> p a b", a=4), lhsT=LT[0:n, 0:n],
                            rhs=rhsU[0:n, hh * 4:(hh + 1) * 4, 0:n], start=True, stop=True),
                            reads=[LTR, rhsUR], writes=[pbR])
                        yield
                        S.op("act", lambda h, n=n, hh=hh, pb=pb: h.activation(
                            out=Eb[0:n, hh * 4:(hh + 1) * 4, 0:n],
                            in_=pb[0:n, 0:4 * n].rearrange("p (a b) -> p a b", a=4), func=AF.Exp),
                            reads=[pbR], writes=[EbR])

                    yield
                S.op("dve", lambda h, n=n, ti=ti, g=g: h.tensor_tensor(
                    out=xw[0:n, :].rearrange("p (a b) -> p a b", a=8),
                    in0=xTM[0:n, ti, g * 512:(g + 1) * 512].rearrange("p (a b) -> p a b", a=8),
                    in1=dteA[0:n, ti, gs].unsqueeze(2).to_broadcast([n, 8, 64]), op=ALU.mult),
                    reads=[xTMR, dteAR], writes=[xwR])
                if need_y:
                    pbo, pboR = B[7]
                    yield
                    S.op("pe", lambda h, n=n, t0=t0, t1=t1, g=g, pbo=pbo: h.matmul(
                        pbo[0:n, :], lhsT=CT[:, g, t0:t1], rhs=hb[:, g * 512:(g + 1) * 512], start=True, stop=True),
                        reads=[CTR, hbRs[g]], writes=[pboR])
                    S.op("dve", lambda h, n=n, gs=gs, pbo=pbo: h.tensor_tensor(
                        out=ytmp[0:n, :].rearrange("p (a b) -> p a b", a=8),
                        in0=pbo[0:n, :].rearrange("p (a b) -> p a b", a=8),
                        in1=ecumA[0:n, ti, gs].unsqueeze(2).to_broadcast([n, 8, 64]), op=ALU.mult),
                        reads=[pboR, ecumR], writes=[ytmpR])
                yield
                S.op("dve", lambda h, g=g, gs=gs: h.tensor_tensor(
                    out=hs[:, g * 512:(g + 1) * 512].rearrange("p (a b) -> p a b", a=8),
                    in0=hs[:, g * 512:(g + 1) * 512].rearrange("p (a b) -> p a b", a=8),
                    in1=cdrA[:, ti, gs].unsqueeze(2).to_broadcast([128, 8, 64]), op=ALU.mult),
                    reads=[hsRs[g], cdrR], writes=[hsRs[g]])
                pbs, pbsR = B[4]
                yield
                S.op("pe", lambda h, n=n, ti=ti, g=g, pbs=pbs: h.matmul(
                    pbs[:, :], lhsT=BTM[0:n, ti, g * 128:(g + 1) * 128], rhs=xw[0:n, :], start=True, stop=True),
                    reads=[BTMR, xwR], writes=[pbsR])
                S.op("dve", lambda h, g=g, pbs=pbs: h.tensor_tensor(
                    out=hs[:, g * 512:(g + 1) * 512], in0=hs[:, g * 512:(g + 1) * 512], in1=pbs[:, :], op=ALU.add),
                    reads=[hsRs[g], pbsR], writes=[hsRs[g]])
                yield
                cp("act", hb[:, g * 512:(g + 1) * 512], hs[:, g * 512:(g + 1) * 512], [hsRs[g]], [hbRs[g]])
                if need_y:
                    pbc, pbcR = B[5]
                    yield
                    S.op("pe", lambda h, n=n, t0=t0, t1=t1, g=g, pbc=pbc: h.matmul(
                        pbc[0:n, 0:n], lhsT=BT[:, g, t0:t1], rhs=CT[:, g, t0:t1], start=True, stop=True),
                        reads=[BTR, CTR], writes=[pbcR])
                    S.op("dve", lambda h, n=n, pbc=pbc: h.tensor_tensor(out=cbm[0:n, 0:n], in0=pbc[0:n, 0:n],
                                                                        in1=U[0:n, 0:n], op=ALU.mult),
                         reads=[pbcR, UR], writes=[cbmR])
                    yield
                    S.op("dve", lambda h, n=n: h.tensor_tensor(
                        out=wgt[0:n, :, 0:n], in0=Eb[0:n, :, 0:n],
                        in1=cbm[0:n, 0:n].unsqueeze(1).to_broadcast([n, 8, n]), op=ALU.mult),
                        reads=[EbR, cbmR], writes=[wgtR])
                    yield
                    S.op("dve", lambda h, n=n, ti=ti, g=g, gs=gs: h.tensor_tensor(
                        out=xdt[0:n, :].rearrange("p (a b) -> p a b", a=8),
                        in0=xTM[0:n, ti, g * 512:(g + 1) * 512].rearrange("p (a b) -> p a b", a=8),
                        in1=dtv[0:n, ti, gs].unsqueeze(2).to_broadcast([n, 8, 64]), op=ALU.mult),
                        reads=[xTMR, dtvR], writes=[xdtR])
                    pbd, pbdR = B[6]

                    def ydiag(h, n=n, ti=ti, g=g, pbd=pbd):
                        for hd in range(8):
                            h.matmul(pbd[0:n, hd * 64:(hd + 1) * 64], lhsT=wgt[0:n, hd, 0:n],
                                     rhs=xdt[0:n, hd * 64:(hd + 1) * 64], start=True, stop=False)
                            r = h.matmul(pbd[0:n, hd * 64:(hd + 1) * 64], lhsT=IDm[0:n, g * 8 + hd, 0:n],
                                         rhs=xTM[0:n, ti, g * 512 + hd * 64:g * 512 + (hd + 1) * 64],
                                         start=False, stop=True)
                        return r
                    yield
                    S.op("pe", ydiag, reads=[wgtR, xdtR, IDmR, xTMR], writes=[pbdR])
                    S.op("dve", lambda h, n=n, pbd=pbd: h.tensor_tensor(out=ytmp[0:n, :], in0=pbd[0:n, :],
                                                                        in1=ytmp[0:n, :], op=ALU.add),
                         reads=[pbdR, ytmpR], writes=[ytmpR])
                    yield
                    S.op("dve", lambda h, n=n, g=g: h.tensor_tensor(out=yg[0:n, g, :], in0=ytmp[0:n, :],
                                                                    in1=zt[0:n, g * 512:(g + 1) * 512], op=ALU.mult),
                         reads=[ytmpR, ztR], writes=[ygR])
                    yield
                    S.op("act", lambda h, n=n, g=g: h.activation(out=ytmp[0:n, :], in_=yg[0:n, g, :], func=AF.Square,
                                                                 scale=1.0 / math.sqrt(512.0),
                                                                 accum_out=ssq[0:n, g:g + 1]),
                         reads=[ygR], writes=[ytmpR, ssqR])


            for si, (s0, sn) in enumerate(seqs):
                if kind == "ctx":
                    S.op("dve", lambda h: h.memset(hs, 0.0), writes=hsRs)
                    S.op("dve", lambda h: h.memset(hb, 0.0), writes=hbRs)
                elif si == 0:
                    S.dma("sp", lambda h: h.dma_start(out=hs, in_=hs_d), reads=[hs_dR], writes=hsRs)
                    cp("act", hb, hs, hsRs, hbRs)
                else:
                    S.dma("sp", lambda h: h.dma_start(out=stg, in_=sssm.rearrange("(c p) n -> p c n", p=128)),
                          writes=[stgR])
                    for q in range(4):
                        pb, pbR = B[q % 2]

                        def tr(h, q=q, pb=pb):
                            for j in range(4):
                                r = h.transpose(out=pb[:, j * 128:(j + 1) * 128], in_=stg[:, q * 4 + j, :], identity=identf)
                            return r
                        S.op("pe", tr, reads=[stgR, identfR], writes=[pbR])
                        cp("act", hs[:, q * 512:(q + 1) * 512], pb[:, :], [pbR], [hsRs[q]])
                    cp("act", hb, hs, hsRs, hbRs)

                def tile_end(n, ti, t0, t1):
                    yg, ygR = ygs[ti % 2]; ssq, ssqR = ssqs[ti % 2]; ynb, ynbR = ynbs[ti % 2]; ybst, ybstR = ybsts[ti % 2]
                    S.op("act", lambda h: h.activation(out=ssq[0:n, 4:8], in_=ssq[0:n, 0:4], func=AF.Ln, bias=EPS),
                         reads=[ssqR], writes=[ssqR])
                    yield
                    S.op("act", lambda h: h.activation(out=ssq[0:n, 4:8], in_=ssq[0:n, 4:8], func=AF.Exp, scale=-0.5), reads=[ssqR], writes=[ssqR])
                    yield
                    S.op("dve", lambda h: h.tensor_tensor(
                        out=ynb[0:n, :].rearrange("p (a b) -> p a b", a=4), in0=yg[0:n, :, :],
                        in1=ssq[0:n, 4:8].unsqueeze(2).to_broadcast([n, 4, 512]), op=ALU.mult),
                        reads=[ygR, ssqR], writes=[ynbR])
                    for half in range(2):
                        pb, pbR = B[6 + half]
                        yield

                        def tr(h, half=half, pb=pb):
                            for j in range(8):
                                r = h.transpose(out=bf(pb)[:, j * 128:j * 128 + n],
                                                in_=ynb[0:n, (half * 8 + j) * 128:(half * 8 + j + 1) * 128],
                                                identity=ident[0:n, 0:n])
                            return r
                        S.op("pe", tr, reads=[ynbR, identR], writes=[pbR])
                        S.op("dve", lambda h, half=half, pb=pb: h.tensor_tensor(
                            out=ybst[:, half * 8:(half + 1) * 8, 0:n],
                            in0=bf(pb).rearrange("p (a b) -> p a b", a=8)[:, :, 0:n],
                            in1=g_ssm[:, half * 8:(half + 1) * 8].unsqueeze(2).to_broadcast([128, 8, n]), op=ALU.mult),
                            reads=[pbR, g_ssmR], writes=[ybstR])
                    yield
                    S.dma("sp", lambda h: h.dma_start(out=ybT_d[:, :, t0:t1], in_=ybst[:, :, 0:n]),
                          reads=[ybstR], writes=[ybT_dR])

                for (t0, t1) in tiles(s0, s0 + sn, 128):
                    n = t1 - t0
                    ti = t0 // 128
                    if need_y:
                        pb, pbR = B[4 + ti % 2]
                        S.op("pe", lambda h, n=n, ti=ti, pb=pb: h.matmul(pb[0:n, 0:32], lhsT=U[0:n, 0:n], rhs=dta[0:n, ti, :],
                                                                          start=True, stop=True),
                             reads=[UR, dtaR], writes=[pbR])
                        S.op("act", lambda h, n=n, ti=ti, pb=pb: h.activation(out=ecumA[0:n, ti, :], in_=pb[0:n, 0:32], func=AF.Exp),
                             reads=[pbR], writes=[ecumR])
                    pb, pbR = B[6 + ti % 2]
                    S.op("pe", lambda h, n=n, ti=ti, pb=pb: h.matmul(pb[:, 0:32], lhsT=onesf[0:n, :], rhs=dta[0:n, ti, :],
                                                                      start=True, stop=True),
                         reads=[onesfR, dtaR], writes=[pbR])
                    S.op("act", lambda h, ti=ti, pb=pb: h.activation(out=cdrA[:, ti, :], in_=pb[:, 0:32], func=AF.Exp),
                         reads=[pbR], writes=[cdrR])
                    pb, pbR = B[ti % 2]
                    S.op("pe", lambda h, n=n, ti=ti, pb=pb: h.matmul(pb[0:n, 0:32], lhsT=LT[0:n, 0:n], rhs=dta[0:n, ti, :],
                                                                      start=True, stop=True),
                         reads=[LTR, dtaR], writes=[pbR])
                    S.op("act", lambda h, n=n, ti=ti, pb=pb: h.activation(out=dteA[0:n, ti, :], in_=pb[0:n, 0:32], func=AF.Exp),
                         reads=[pbR], writes=[dteAR])
                    S.op("dve", lambda h, n=n, ti=ti: h.tensor_tensor(out=dteA[0:n, ti, :], in0=dteA[0:n, ti, :],
                                                                      in1=dtv[0:n, ti, :], op=ALU.mult),
                         reads=[dteAR, dtvR], writes=[dteAR])
                pend = None
                for (t0, t1) in tiles(s0, s0 + sn, 128):
                    n = t1 - t0
                    ti = t0 // 128
                    if need_y:
                        zt, ztR = zts[ti % 2]
                        S.dma("sp", lambda h, t0=t0, t1=t1, n=n, zt=zt: h.dma_start(out=zt[0:n, :], in_=zs_d[t0:t1, :]),
                              reads=[dR["zs_d"]], writes=[ztR])
                    for pair in ((0, 1), (2, 3)):
                        gens = [do_group(g, n, ti, t0, t1) for g in pair]
                        for _ in range(3):
                            next(gens[0])
                        if pend is not None:
                            gens.append(pend)
                            pend = None
                        while gens:
                            for gg in list(gens):
                                try:
                                    next(gg)
                                except StopIteration:
                                    gens.remove(gg)
                    if need_y:
                        pend = tile_end(n, ti, t0, t1)
                if pend is not None:
                    for _ in pend:
                        pass
                if kind == "ctx":
                    S.op("dve", lambda h: h.tensor_scalar(out=hs, in0=hs, scalar1=flg[:, 0:1], scalar2=None, op0=ALU.mult),
                         reads=hsRs + [flgR], writes=hsRs)
                    S.dma("sp", lambda h: h.dma_start(out=hs_d, in_=hs), reads=hsRs, writes=[hs_dR])
                    S.dma("sp", lambda h: h.dma_start(out=tail_d, in_=st0), reads=[st0R], writes=[tail_dR])
                else:
                    odst, odR = (ssm_p, dR["ssm_p"]) if si == 0 else (ssm_s, dR["ssm_s"])
                    for q in range(4):
                        pb, pbR = B[q % 2]

                        def tr(h, q=q, pb=pb):
                            for j in range(4):
                                r = h.transpose(out=pb[:, j * 128:(j + 1) * 128], in_=hs[:, (q * 4 + j) * 128:(q * 4 + j + 1) * 128],
                                                identity=identf)
                            return r
                        S.op("pe", tr, reads=[hsRs[q], identfR], writes=[pbR])
                        cp("act", stg[:, q * 4:(q + 1) * 4, :], pb[:, :].rearrange("p (a b) -> p a b", a=4), [pbR], [stgR])
                    S.dma("sp", lambda h, odst=odst: h.dma_start(out=odst.rearrange("(c p) n -> p c n", p=128), in_=stg),
                          reads=[stgR], writes=[odR])
                    cdst, cdR_ = (conv_p, dR["conv_p"]) if si == 0 else (conv_s, dR["conv_s"])
                    sto, stoR = (st0, st0R) if si == 0 else (st1, st1R)
                    pb, pbR = B[2]
                    S.op("pe", lambda h, sto=sto, pb=pb: h.transpose(out=pb[0:72, 0:128], in_=sto.rearrange("p k b -> p (k b)"),
                                                                     identity=identf), reads=[stoR, identfR], writes=[pbR])
                    cp("act", stg[0:72, 0, :], pb[0:72, 0:128], [pbR], [stgR])
                    S.dma("sp", lambda h, cdst=cdst: h.dma_start(out=cdst.rearrange("k (b p) -> (k b) p", p=128), in_=stg[0:72, 0, :]),
                          reads=[stgR], writes=[cdR_])
            AR.release(m)
            S.fence()
        def kv_proj(T, toff, kTb, kTbR, vSb, vSbR, kiTb, kiTbR, koff, outs):
            def epi_kv(ps, psR, c0, ncol, t0, t1):
                n = t1 - t0
                if outs is None:
                    if c0 == 256:
                        cp("act", vSb[0:n, (koff + t0) // 128, :], ps, [psR], [vSbR])
                else:
                    stf, stfR = cx["kvst"][kvi[0] % 2]
                    kvi[0] += 1
                    cp("dve", stf[0:n, 0:ncol], ps, [psR], [stfR])
                    if c0 == 256:
                        cp("act", vSb[0:n, (koff + t0) // 128, :], stf[0:n, 0:ncol], [stfR], [vSbR])
                    for (lo, hi, od, odR) in outs:
                        if lo <= t0 < hi:
                            dd = od[0] if c0 == 0 else od[1]
                            S.dma("sp", lambda h, dd=dd, t0=t0, lo=lo, n=n, stf=stf, ncol=ncol: h.dma_start(
                                out=dd[t0 - lo:t0 - lo + n, :], in_=stf[0:n, 0:ncol]), reads=[stfR], writes=[odR])
            gemm(hT, hTR, 16, w_in, COL["a_k"], 512, "TM", tiles(0, T, 128), epi_kv, B[0:4])

            def epi_kT(ps, psR, c0, ns, t0, t1):
                cp("act", kTb[:, c0 // 128, koff + t0:koff + t1], ps, [psR], [kTbR])
            gemm(hT, hTR, 16, w_in, COL["a_k"], 256, "FM", tiles(0, T, 512), epi_kT, B[0:4])

            def epi_kiT(ps, psR, c0, ns, t0, t1):
                cp("act", kiTb[:, koff + t0:koff + t1], ps, [psR], [kiTbR])
            gemm(hT, hTR, 16, w_in, COL["i_k"], 64, "FM", tiles(0, T, 512), epi_kiT, B[0:4], dupcols=2)
            if outs is not None:
                def epi_ki(ps, psR, c0, ncol, t0, t1):
                    n = t1 - t0
                    stf, stfR = cx["kvst"][kvi[0] % 2]
                    kvi[0] += 1
                    cp("dve", stf[0:n, 0:80], ps[:, 0:80], [psR], [stfR])
                    cp("act", cx["iw"][0:n, t0 // 128, :], stf[0:n, 64:80], [stfR], [cx["iwR"]])
                    for (lo, hi, od, odR) in outs:
                        if lo <= t0 < hi:
                            S.dma("sp", lambda h, od=od, t0=t0, lo=lo, n=n, stf=stf: h.dma_start(
                                out=od[2][t0 - lo:t0 - lo + n, :], in_=stf[0:n, 0:64]), reads=[stfR], writes=[odR])
                gemm(hT, hTR, 16, w_in, COL["i_k"], 80, "TM", tiles(0, T, 128), epi_ki, B[0:4])

        kvi = [0]
        cx = {}

        def ctx_kv_phase():
            global_m = AR.mark()
            kTc, kTcR = AR.alloc([2, TC], BF16, "kTc")
            vSc, vScR = AR.alloc([8, 256], BF16, "vSc")
            kiTc, kiTcR = AR.alloc([TC], BF16, "kiTc")
            kv_proj(TC, 0, kTc, kTcR, vSc, vScR, kiTc, kiTcR, 0, None)
            S.dma("sp", lambda h: h.dma_start(out=kT_d, in_=kTc), reads=[kTcR], writes=[kT_dR])
            S.dma("sp", lambda h: h.dma_start(out=vS_d, in_=vSc), reads=[vScR], writes=[vS_dR])
            S.dma("sp", lambda h: h.dma_start(out=kiT_d, in_=kiTc), reads=[kiTcR], writes=[kiT_dR])
            AR.release(global_m)
            S.fence()

        def attention_phase():
            m = AR.mark()
            L = TC + TO
            Ls = 1024 + TS
            kT, kTR = AR.alloc([2, L + TS], BF16, "kT")
            vS, vSR = AR.alloc([17, 256], BF16, "vS")
            kiT, kiTR = AR.alloc([L + TS], BF16, "kiT")
            kTs, kTsR = AR.alloc([2, 1024], BF16, "kTs")
            vSs, vSsR = AR.alloc([8, 256], BF16, "vSs")
            kiTs, kiTsR = AR.alloc([1024], BF16, "kiTs")
            qT, qTR = AR.alloc([8, TT], BF16, "qT")
            iqT, iqTR = AR.alloc([8, TT], BF16, "iqT")
            yaT, yaTR = AR.alloc([8, TT], BF16, "yaT")
            iw, iwR = AR.alloc([9, 16], F32, "iw")
            cx["iw"], cx["iwR"] = iw, iwR
            cx["kvst"] = [AR.alloc([256], F32, f"kvst{i}") for i in range(2)]
            S.dma("sp", lambda h: h.dma_start(out=kT[:, :, 0:TC], in_=kT_d), reads=[kT_dR], writes=[kTR])
            S.dma("sp", lambda h: h.dma_start(out=vS[:, 0:8, :], in_=vS_d), reads=[vS_dR], writes=[vSR])
            S.dma("sp", lambda h: h.dma_start(out=kiT[:, 0:TC], in_=kiT_d), reads=[kiT_dR], writes=[kiTR])
            outs = [(0, TO, (k_own[:, :], v_own[:, :], ki_own), dR["k_own"]),
                    (TO, TT, (k_s[:, :], v_s[:, :], ki_s), dR["k_s"])]
            if ATT_PRO >= 1:
                kv_proj(TT, 0, kT, kTR, vS, vSR, kiT, kiTR, TC, outs)
            ldf, ldfR = AR.alloc([256], F32, "ldf")
            ldb, ldbR = AR.alloc([256], BF16, "ldb")
            ldi, ldiR = AR.alloc([128], F32, "ldi")
            ldib, ldibR = AR.alloc([128], BF16, "ldib")
            for t in range(8 if ATT_PRO >= 2 else 0):
                S.dma("sp", lambda h, t=t: h.dma_start(out=ldf, in_=cv[t * 128:(t + 1) * 128, :]), writes=[ldfR])
                cp("dve", vSs[:, t, :], ldf, [ldfR], [vSsR])
                S.dma("sp", lambda h, t=t: h.dma_start(out=ldf, in_=ck[t * 128:(t + 1) * 128, :]), reads=[ldfR], writes=[ldfR])
                cp("dve", ldb, ldf, [ldfR], [ldbR])
                S.dma("sp", lambda h, t=t: h.dma_start(out=ldi[:, 0:64], in_=cki[t * 128:(t + 1) * 128, :]), writes=[ldiR])
                S.dma("sp", lambda h, t=t: h.dma_start(out=ldi[:, 64:128], in_=cki[t * 128:(t + 1) * 128, :]), reads=[ldiR], writes=[ldiR])
                cp("dve", ldib, ldi, [ldiR], [ldibR])
                pb, pbR = B[t % 2]

                def tr(h, pb=pb):
                    h.transpose(out=bf(pb)[:, 0:128], in_=ldb[:, 0:128], identity=ident)
                    h.transpose(out=bf(pb)[:, 128:256], in_=ldb[:, 128:256], identity=ident)
                    return h.transpose(out=bf(pb)[:, 256:384], in_=ldib, identity=ident)
                S.op("pe", tr, reads=[ldbR, ldibR, identR], writes=[pbR])
                cp("act", kTs[:, :, t * 128:(t + 1) * 128], bf(pb)[:, 0:256].rearrange("p (a b) -> p a b", a=2), [pbR], [kTsR])
                cp("act", kiTs[:, t * 128:(t + 1) * 128], bf(pb)[:, 256:384], [pbR], [kiTsR])

            def epi_q(dst, dstR):
                def f(ps, psR, c0, ns, t0, t1):
                    cp("act", dst[:, c0 // 128, t0:t1], ps, [psR], [dstR])
                return f
            if ATT_PRO >= 3:
                gemm(hT, hTR, 16, w_in, COL["a_q"], 1024, "FM", tiles(0, TT, 512), epi_q(qT, qTR), B[0:4])
                gemm(hT, hTR, 16, w_in, COL["i_q"], 1024, "FM", tiles(0, TT, 512), epi_q(iqT, iqTR), B[0:4])

            scs = [AR.alloc([L], F32, f"sc{i}") for i in range(2)]
            rrs = [AR.alloc([512], F32, f"rr{i}") for i in range(3)]
            jm, jkR = AR.alloc([2 * L], BF16, "jm")
            mkR = Res("mk")
            jk32 = jm.bitcast(F32)
            mk = jm[:, L:2 * L]
            mkTs = [AR.alloc([17, 128], BF16, f"mkT{i}") for i in range(2)]
            ees = [AR.alloc([512], BF16, f"ee{i}") for i in range(3)]
            pTs = [AR.alloc([512], BF16, f"pT{i}") for i in range(3)]
            rden, rdenR = AR.alloc([512], F32, "rden")
            bss = [AR.alloc([8], F32, f"bs{i}") for i in range(2)]

            def q_tile(q0, nq, segs, bias_ctx, last_mask, stage):
                qi = q0 // 128
                sc, scR = scs[qi % 2]
                mkT, mkTR = mkTs[qi % 2]
                bs, bsR = bss[qi % 2]
                if stage >= 1:
                    Lk = sum(sg[7] for sg in segs)
                kcol = 0
                if stage >= 1:
                    segs_idx = []
                else:
                    segs_idx = segs
                kmap = []
                for sg in segs_idx:
                    for c in range(0, sg[7], 512):
                        n = min(512, sg[7] - c)
                        kmap.append((kcol, n, sg, sg[6] + c))
                        kcol += n
                if stage == 0:
                    Lk = kcol
                units = [(s0, n, sg, c0, hd) for (s0, n, sg, c0) in kmap for hd in range(16)]

                def idx_front(u):
                    s0, n, sg, c0, hd = units[u]
                    pb, pbR = B[u % 2]
                    rr, rrR = rrs[u % 3]
                    half = (hd % 2) * 64
                    S.op("pe", lambda h: h.matmul(
                        pb[0:nq, 0:n], lhsT=iqT[half:half + 64, hd // 2, q0:q0 + nq],
                        rhs=sg[2][half:half + 64, c0:c0 + n], start=True, stop=True),
                        reads=[iqTR, sg[3]], writes=[pbR])
                    S.op("act", lambda h: h.activation(out=rr[0:nq, 0:n], in_=pb[0:nq, 0:n], func=AF.Relu),
                         reads=[pbR], writes=[rrR])

                def idx_back(u):
                    s0, n, sg, c0, hd = units[u]
                    rr, rrR = rrs[u % 3]
                    if hd == 0:
                        S.op(IDX_ENG, lambda h: h.tensor_scalar(
                            out=sc[0:nq, s0:s0 + n], in0=rr[0:nq, 0:n], scalar1=iw[0:nq, qi, 0:1], scalar2=None,
                            op0=ALU.mult), reads=[rrR, iwR], writes=[scR])
                    else:
                        S.op(IDX_ENG, lambda h: h.scalar_tensor_tensor(
                            out=sc[0:nq, s0:s0 + n], in0=rr[0:nq, 0:n], scalar=iw[0:nq, qi, hd:hd + 1],
                            in1=sc[0:nq, s0:s0 + n], op0=ALU.mult, op1=ALU.add), reads=[rrR, iwR, scR], writes=[scR])
                LOOK = 2
                for u in range(min(LOOK, len(units))):
                    idx_front(u)
                for u in range(len(units)):
                    if u + LOOK < len(units):
                        idx_front(u + LOOK)
                    idx_back(u)
                    yield
                if stage == 0:
                    if bias_ctx:
                        S.op(IDX_ENG, lambda h: h.tensor_scalar(out=sc[0:nq, 0:TC], in0=sc[0:nq, 0:TC], scalar1=flg[0:nq, 1:2],
                                                                scalar2=None, op0=ALU.add), reads=[scR, flgR], writes=[scR])
                    if last_mask:
                        S.op(IDX_ENG, lambda h: h.tensor_tensor(out=sc[0:nq, Lk - 128:Lk], in0=sc[0:nq, Lk - 128:Lk],
                                                                in1=cmask[0:nq, :], op=ALU.add), reads=[scR, cmaskR], writes=[scR])
                    return
                if stage == 1:
                    S.op("dve", lambda h: h.memset(bs[0:nq, 1:2], BIS_LO + BIS_W / 2.0), writes=[bsR])
                    for it in range(NBIS):
                        step = BIS_W / (2.0 ** (it + 1))
                        S.op("dve", lambda h: h.tensor_scalar(out=jk32[0:nq, 0:Lk], in0=sc[0:nq, 0:Lk], scalar1=bs[0:nq, 1:2],
                                                              scalar2=0.0, op0=ALU.is_ge, op1=ALU.add,
                                                              accum_out=bs[0:nq, 2:3]), reads=[scR, bsR], writes=[jkR, mkR, bsR])
                        S.op("dve", lambda h, step=step: h.tensor_scalar(out=bs[0:nq, 3:4], in0=bs[0:nq, 2:3],
                                                                         scalar1=float(TOPK) - 0.5, scalar2=step,
                                                                         op0=ALU.is_ge, op1=ALU.mult), reads=[bsR], writes=[bsR])
                        nstep = step / 2.0 if it < NBIS - 1 else step
                        S.op("dve", lambda h, nstep=nstep: h.scalar_tensor_tensor(
                            out=bs[0:nq, 1:2], in0=bs[0:nq, 3:4], scalar=nstep, in1=bs[0:nq, 1:2], op0=ALU.subtract, op1=ALU.add),
                            reads=[bsR], writes=[bsR])
                        yield
                    S.op("dve", lambda h: h.tensor_scalar(out=mk[0:nq, 0:Lk], in0=sc[0:nq, 0:Lk], scalar1=bs[0:nq, 1:2],
                                                          scalar2=None, op0=ALU.is_ge), reads=[scR, bsR], writes=[mkR])
                nkt = (Lk + 127) // 128
                for b0 in (range(0, nkt, 8) if stage == 1 else []):
                    pb, pbR = B[2 + (b0 // 8) % 2]
                    cnt = min(8, nkt - b0)

                    def tr(h, b0=b0, cnt=cnt, pb=pb):
                        for j in range(cnt):
                            kk = (b0 + j) * 128
                            nk = min(128, Lk - kk)
                            r = h.transpose(out=bf(pb)[0:nk, j * 128:j * 128 + nq], in_=mk[0:nq, kk:kk + nk],
                                            identity=ident[0:nq, 0:nq])
                        return r
                    S.op("pe", tr, reads=[mkR, identR], writes=[pbR])
                    for j in range(cnt):
                        kk = (b0 + j) * 128
                        nk = min(128, Lk - kk)
                        cp("act", mkT[0:nk, b0 + j, 0:nq], bf(pb)[0:nk, j * 128:j * 128 + nq], [pbR], [mkTR])
                if stage == 1:
                    return
                for g in range(2):
                    po, poR = B[4]
                    pd, pdR = B[5]
                    kts = []
                    kc = 0
                    for sg in segs:
                        for c in range(0, sg[7], 128):
                            nk = min(128, sg[7] - c)
                            kts.append((kc // 128, nk, sg, sg[6] + c, sg[8] + c // 128))
                            kc += nk
                    def front(idx, g=g):
                        kt, nk, sg, c0, vt = kts[idx]
                        pb, pbR = B[6 + idx % 2]
                        ee, eeR = ees[idx % 3]

                        def smm(h):
                            for a in range(4):
                                r = h.matmul(pb[0:nk, a * nq:(a + 1) * nq], lhsT=sg[0][:, g, c0:c0 + nk],
                                             rhs=qT[:, g * 4 + a, q0:q0 + nq], start=True, stop=True)
                            return r
                        S.op("pe", smm, reads=[sg[1], qTR], writes=[pbR])
                        S.op("act", lambda h: h.activation(out=ee[0:nk, 0:4 * nq], in_=pb[0:nk, 0:4 * nq],
                                                           func=AF.Exp, scale=1.0 / math.sqrt(128.0)),
                             reads=[pbR], writes=[eeR])

                    def back(idx, g=g, po=po, pd=pd, poR=poR, pdR=pdR):
                        kt, nk, sg, c0, vt = kts[idx]
                        ee, eeR = ees[idx % 3]
                        pT, pTR = pTs[idx % 3]
                        S.op("dve", lambda h: h.tensor_tensor(
                            out=pT[0:nk, 0:4 * nq].rearrange("p (a b) -> p a b", a=4),
                            in0=ee[0:nk, 0:4 * nq].rearrange("p (a b) -> p a b", a=4),
                            in1=mkT[0:nk, kt, 0:nq].unsqueeze(1).to_broadcast([nk, 4, nq]), op=ALU.mult),
                            reads=[eeR, mkTR], writes=[pTR])
                        first = idx == 0
                        lastk = idx == len(kts) - 1
                        S.op("pe", lambda h: h.matmul(
                            po[:, 0:4 * nq], lhsT=sg[4][0:nk, vt, g * 128:(g + 1) * 128], rhs=pT[0:nk, 0:4 * nq],
                            start=first, stop=lastk), reads=[sg[5], pTR], writes=[poR])
                        S.op("pe", lambda h: h.matmul(
                            pd[:, 0:4 * nq], lhsT=onesb[0:nk, :], rhs=pT[0:nk, 0:4 * nq], start=first, stop=lastk),
                            reads=[onesbR, pTR], writes=[pdR])
                    front(0)
                    for idx in range(len(kts)):
                        if idx + 1 < len(kts):
                            front(idx + 1)
                        back(idx)
                        yield
                    S.op("act", lambda h, pd=pd: h.activation(out=rden[:, 0:4 * nq], in_=pd[:, 0:4 * nq], func=AF.Ln),
                         reads=[pdR], writes=[rdenR])
                    S.op("act", lambda h: h.activation(out=rden[:, 0:4 * nq], in_=rden[:, 0:4 * nq], func=AF.Exp, scale=-1.0),
                         reads=[rdenR], writes=[rdenR])
                    S.op("dve", lambda h, po=po, g=g: h.tensor_tensor(
                        out=yaT[:, g * 4:(g + 1) * 4, q0:q0 + nq], in0=po[:, 0:4 * nq].rearrange("p (a b) -> p a b", a=4),
                        in1=rden[:, 0:4 * nq].rearrange("p (a b) -> p a b", a=4), op=ALU.mult),
                        reads=[poR, rdenR], writes=[yaTR])

            jobs = []
            for i in range(8):
                segs = [(kT, kTR, kiT, kiTR, vS, vSR, 0, TC + (i + 1) * 128, 0)]
                jobs.append((i * 128, 128, segs, True, True))
            segs = [(kTs, kTsR, kiTs, kiTsR, vSs, vSsR, 0, 1024, 0),
                    (kT, kTR, kiT, kiTR, vS, vSR, TC + TO, TS, 16)]
            jobs.append((TO, TS, segs, False, False))
            def nyield(job, stage):
                Lk_ = sum(sg[7] for sg in job[2])
                if stage == 0:
                    return 16 * ((Lk_ + 511) // 512)
                if stage == 1:
                    return NBIS
                return 2 * ((Lk_ + 127) // 128)
            nj = len(jobs)
            for st_ in range(nj + 2):
                act = []
                for stage, ji in ((2, st_ - 2), (1, st_ - 1), (0, st_)):
                    if 0 <= ji < nj:
                        act.append([q_tile(*jobs[ji], stage), nyield(jobs[ji], stage)])
                rounds = 30
                quota = [(-(-a[1] // rounds)) for a in act]
                while act:
                    for k_ in range(len(act) - 1, -1, -1):
                        for _ in range(quota[k_]):
                            try:
                                next(act[k_][0])
                            except StopIteration:
                                act.pop(k_)
                                quota.pop(k_)
                                break

            gt, gtR = AR.alloc([512], BF16, "gt")

            def epi_az(ps, psR, c0, ns, t0, t1):
                S.op("act", lambda h: h.activation(out=gt[:, 0:t1 - t0], in_=ps, func=AF.Silu), reads=[psR], writes=[gtR])
                S.op("dve", lambda h: h.tensor_tensor(out=yaT[:, c0 // 128, t0:t1], in0=yaT[:, c0 // 128, t0:t1],
                                                      in1=gt[:, 0:t1 - t0], op=ALU.mult), reads=[gtR, yaTR], writes=[yaTR])
            if ATT_PRO >= 4:
                gemm(hT, hTR, 16, w_in, COL["a_z"], 1024, "FM", tiles(0, TT, 512), epi_az, B[0:4])
            S.dma("sp", lambda h: h.dma_start(out=yaT_d, in_=yaT), reads=[yaTR], writes=[yaT_dR])
            AR.release(m)
            S.fence()
        def mem_phase():
            m = AR.mark()
            memhT, memhTR = AR.alloc([16, NM], BF16, "memhT")
            mkTp, mkTpR = AR.alloc([8, NM], BF16, "mkTp")
            mvp, mvpR = AR.alloc([2, 1024], BF16, "mvp")
            mkTs, mkTsR = AR.alloc([8, NM], BF16, "mkTs")
            mvs, mvsR = AR.alloc([2, 1024], BF16, "mvs")
            mqT, mqTR = AR.alloc([8, TT], BF16, "mqT")
            ymT, ymTR = AR.alloc([8, TT], BF16, "ymT")
            norm_rows(memx, NM, g_mem, g_memR, memhT, memhTR, 0, B[6:8])
            stf2 = [AR.alloc([256], F32, f"mst{i}") for i in range(2)]
            ci = [0]

            def epi_mk(ps, psR, c0, ncol, t0, t1):
                stf, stfR = stf2[ci[0] % 2]
                ci[0] += 1
                cp("dve", stf[:, 0:ncol], ps, [psR], [stfR])
                S.dma("sp", lambda h: h.dma_start(out=memk_o[t0:t1, c0:c0 + ncol], in_=stf[:, 0:ncol]),
                      reads=[stfR], writes=[dR["memk_o"]])
            gemm(memhT, memhTR, 16, w_mem_kv, 0, 1024, "TM", tiles(0, NM, 128), epi_mk, B[0:4])

            def epi_mv(ps, psR, c0, ncol, t0, t1):
                stf, stfR = stf2[ci[0] % 2]
                ci[0] += 1
                cp("dve", stf[:, 0:ncol], ps, [psR], [stfR])
                cp("act", mvp[:, t0 // 128, c0:c0 + ncol], stf[:, 0:ncol], [stfR], [mvpR])
                S.dma("sp", lambda h: h.dma_start(out=memv_o[t0:t1, c0:c0 + ncol], in_=stf[:, 0:ncol]),
                      reads=[stfR], writes=[dR["memv_o"]])
            gemm(memhT, memhTR, 16, w_mem_kv, 1024, 1024, "TM", tiles(0, NM, 128), epi_mv, B[0:4])

            def epi_mkT(ps, psR, c0, ns, t0, t1):
                cp("act", mkTp[:, c0 // 128, t0:t1], ps, [psR], [mkTpR])
            gemm(memhT, memhTR, 16, w_mem_kv, 0, 1024, "FM", [(0, NM)], epi_mkT, B[0:4])
            ldf, ldfR = AR.alloc([1024], F32, "mldf")
            ldb, ldbR = AR.alloc([1024], BF16, "mldb")
            for t in range(2):
                S.dma("sp", lambda h, t=t: h.dma_start(out=ldf, in_=cmv[t * 128:(t + 1) * 128, :]), writes=[ldfR])
                cp("dve", mvs[:, t, :], ldf, [ldfR], [mvsR])
                S.dma("sp", lambda h, t=t: h.dma_start(out=ldf, in_=cmk[t * 128:(t + 1) * 128, :]), reads=[ldfR], writes=[ldfR])
                cp("dve", ldb, ldf, [ldfR], [ldbR])
                pb, pbR = B[t % 2]

                def tr(h, pb=pb):
                    for j in range(8):
                        r = h.transpose(out=bf(pb)[:, j * 128:(j + 1) * 128], in_=ldb[:, j * 128:(j + 1) * 128], identity=ident)
                    return r
                S.op("pe", tr, reads=[ldbR, identR], writes=[pbR])
                cp("act", mkTs[:, :, t * 128:(t + 1) * 128], bf(pb).rearrange("p (a b) -> p a b", a=8), [pbR], [mkTsR])

            def epi_mq(ps, psR, c0, ns, t0, t1):
                cp("act", mqT[:, c0 // 128, t0:t1], ps, [psR], [mqTR])
            gemm(hT, hTR, 16, w_in, COL["m_q"], 1024, "FM", tiles(0, TT, 512), epi_mq, B[0:4])
            ee, eeR = AR.alloc([2, 512], BF16, "mee")
            rden, rdenR = AR.alloc([512], F32, "mrden")
            for (t0, t1) in tiles(0, TT, 512):
                nq = t1 - t0
                mkT_, mkR_, mv_, mvR_ = (mkTp, mkTpR, mvp, mvpR) if t0 < TO else (mkTs, mkTsR, mvs, mvsR)
                for hd in range(4):
                    for mt in range(2):
                        pb, pbR = B[mt]

                        def smm(h, pb=pb, mt=mt, hd=hd, mkT_=mkT_, t0=t0, t1=t1, nq=nq):
                            for c in range(2):
                                r = h.matmul(pb[:, 0:nq], lhsT=mkT_[:, hd * 2 + c, mt * 128:(mt + 1) * 128],
                                             rhs=mqT[:, hd * 2 + c, t0:t1], start=(c == 0), stop=(c == 1))
                            return r
                        S.op("pe", smm, reads=[mkR_, mqTR], writes=[pbR])
                        S.op("act", lambda h, pb=pb, mt=mt, nq=nq: h.activation(out=ee[:, mt, 0:nq], in_=pb[:, 0:nq], func=AF.Exp,
                                                                         scale=1.0 / 16.0), reads=[pbR], writes=[eeR])
                    pd, pdR = B[2]

                    def dmm(h, pd=pd, nq=nq):
                        for mt in range(2):
                            r = h.matmul(pd[:, 0:nq], lhsT=onesb, rhs=ee[:, mt, 0:nq], start=(mt == 0), stop=(mt == 1))
                        return r
                    S.op("pe", dmm, reads=[onesbR, eeR], writes=[pdR])
                    S.op("act", lambda h, pd=pd, nq=nq: h.activation(out=rden[:, 0:nq], in_=pd[:, 0:nq], func=AF.Ln),
                         reads=[pdR], writes=[rdenR])
                    S.op("act", lambda h, nq=nq: h.activation(out=rden[:, 0:nq], in_=rden[:, 0:nq], func=AF.Exp, scale=-1.0),
                         reads=[rdenR], writes=[rdenR])
                    for c in range(2):
                        po, poR = B[4 + c]

                        def omm(h, po=po, c=c, hd=hd, mv_=mv_, nq=nq):
                            for mt in range(2):
                                r = h.matmul(po[:, 0:nq], lhsT=mv_[:, mt, hd * 256 + c * 128:hd * 256 + (c + 1) * 128],
                                             rhs=ee[:, mt, 0:nq], start=(mt == 0), stop=(mt == 1))
                            return r
                        S.op("pe", omm, reads=[mvR_, eeR], writes=[poR])
                        S.op("dve", lambda h, po=po, c=c, hd=hd, t0=t0, t1=t1, nq=nq: h.tensor_tensor(
                            out=ymT[:, hd * 2 + c, t0:t1], in0=po[:, 0:nq], in1=rden[:, 0:nq], op=ALU.mult),
                            reads=[poR, rdenR], writes=[ymTR])
            gt, gtR = AR.alloc([512], BF16, "mgt")

            def epi_mz(ps, psR, c0, ns, t0, t1):
                S.op("act", lambda h: h.activation(out=gt[:, 0:t1 - t0], in_=ps, func=AF.Silu), reads=[psR], writes=[gtR])
                S.op("dve", lambda h: h.tensor_tensor(out=ymT[:, c0 // 128, t0:t1], in0=ymT[:, c0 // 128, t0:t1],
                                                      in1=gt[:, 0:t1 - t0], op=ALU.mult), reads=[gtR, ymTR], writes=[ymTR])
            gemm(hT, hTR, 16, w_in, COL["m_z"], 1024, "FM", tiles(0, TT, 512), epi_mz, B[0:4])
            S.dma("sp", lambda h: h.dma_start(out=ymT_d, in_=ymT), reads=[ymTR], writes=[ymT_dR])
            AR.release(m)
            S.fence()

        def merge_out_phase():
            m = AR.mark()
            mg, mgR = AR.alloc([16, TT], BF16, "mg")
            m2 = AR.mark()
            ys = {}
            for nm, d_, dR_, nb in (("a", yaT_d, yaT_dR, 8), ("b", ybT_d, ybT_dR, 16), ("m", ymT_d, ymT_dR, 8)):
                t_, tR_ = AR.alloc([nb, TT], BF16, "y" + nm)
                S.dma("sp", lambda h, t_=t_, d_=d_: h.dma_start(out=t_, in_=d_), reads=[dR_], writes=[tR_])
                ys[nm] = (t_, tR_, nb)
            tP, tPR = AR.alloc([2, TT], F32, "tP")
            tG, tGR = AR.alloc([512], F32, "tG")
            for bi_, (nm, W) in enumerate((("a", w_pa), ("b", w_pb), ("m", w_pm))):
                yT_, yR_, nb = ys[nm]
                for c0 in range(0, D, 256):
                    def epi_p(ps, psR, cc, ns, t0, t1):
                        cp("act", tP[:, cc // 128, t0:t1], ps, [psR], [tPR])
                    gemm(yT_, yR_, nb, W, c0, 256, "FM", tiles(0, TT, 512), epi_p, B[0:4])

                    def epi_g(ps, psR, cc, ns, t0, t1, c0=c0, bi_=bi_):
                        blk = (c0 + cc) // 128
                        n = t1 - t0
                        S.op("act", lambda h: h.activation(out=tG[:, 0:n], in_=ps, func=AF.Sigmoid), reads=[psR], writes=[tGR])
                        if bi_ == 0:
                            S.op("dve", lambda h: h.tensor_tensor(out=mg[:, blk, t0:t1], in0=tG[:, 0:n], in1=tP[:, cc // 128, t0:t1],
                                                                  op=ALU.mult), reads=[tGR, tPR], writes=[mgR])
                        else:
                            S.op("dve", lambda h: h.tensor_tensor(out=tG[:, 0:n], in0=tG[:, 0:n], in1=tP[:, cc // 128, t0:t1],
                                                                  op=ALU.mult), reads=[tGR, tPR], writes=[tGR])
                            S.op("dve", lambda h: h.tensor_tensor(out=mg[:, blk, t0:t1], in0=mg[:, blk, t0:t1], in1=tG[:, 0:n],
                                                                  op=ALU.add), reads=[tGR, mgR], writes=[mgR])
                    gemm(hT, hTR, 16, w_in, COL["gates"] + bi_ * D + c0, 256, "FM", tiles(0, TT, 512), epi_g, B[4:8])
            AR.release(m2)
            S.fence()
            res, resR = AR.alloc([9, D], F32, "res")
            gf, gfR = AR.alloc([D], F32, "gf")
            S.dma("sp", lambda h: h.dma_start(out=gf, in_=norm_final.partition_broadcast(128)), writes=[gfR])
            for ti in range(8):
                S.dma("sp", lambda h, ti=ti: h.dma_start(out=res[:, ti, :], in_=x_own[ti * 128:(ti + 1) * 128, :]), writes=[resR])
            S.dma("sp", lambda h: h.dma_start(out=res[0:TS, 8, :], in_=x_smp[:, :]), reads=[resR], writes=[resR])

            def epi_o(ps, psR, c0, ncol, t0, t1):
                n = t1 - t0
                S.op("dve", lambda h: h.tensor_tensor(out=res[0:n, t0 // 128, c0:c0 + ncol], in0=res[0:n, t0 // 128, c0:c0 + ncol],
                                                      in1=ps, op=ALU.add), reads=[psR, resR], writes=[resR])
            gemm(mg, mgR, 16, w_o, 0, D, "TM", tiles(0, TT, 128), epi_o, B[0:4])
            jk, jkR = AR.alloc([D], BF16, "ojk")
            for ti, (t0, t1) in enumerate(tiles(0, TT, 128)):
                n = t1 - t0
                S.op("act", lambda h, n=n, ti=ti: h.activation(out=jk[0:n, :], in_=res[0:n, ti, :], func=AF.Square,
                                                               scale=1.0 / math.sqrt(D), accum_out=sm[0:n, 4:5]),
                     reads=[resR], writes=[jkR, smR])
                S.op("act", lambda h, n=n: h.activation(out=sm[0:n, 5:6], in_=sm[0:n, 4:5], func=AF.Ln, bias=EPS),
                     reads=[smR], writes=[smR])
                S.op("act", lambda h, n=n: h.activation(out=sm[0:n, 6:7], in_=sm[0:n, 5:6], func=AF.Exp, scale=-0.5), reads=[smR], writes=[smR])
                S.op("dve", lambda h, n=n, ti=ti: h.scalar_tensor_tensor(out=res[0:n, ti, :], in0=res[0:n, ti, :], scalar=sm[0:n, 6:7],
                                                                         in1=gf[0:n, :], op0=ALU.mult, op1=ALU.mult),
                     reads=[resR, smR, gfR], writes=[resR])
                if t0 < TO:
                    S.dma("sp", lambda h, n=n, ti=ti, t0=t0: h.dma_start(out=y_own[t0:t0 + n, :], in_=res[0:n, ti, :]),
                          reads=[resR], writes=[dR["y_own"]])
                else:
                    S.dma("sp", lambda h, n=n, ti=ti: h.dma_start(out=y_smp[:, :], in_=res[0:n, ti, :]),
                          reads=[resR], writes=[dR["y_smp"]])
            AR.release(m)

        steps = [
            lambda: norm_rows(x_ctx, TC, g_in, g_inR, hT, hTR, 0, B[6:8]),
            ctx_kv_phase,
            lambda: ssm_pass(TC, [(0, TC)], False, "ctx"),
            lambda: (norm_rows(x_own, TO, g_in, g_inR, hT, hTR, 0, B[6:8]),
                     norm_rows(x_smp, TS, g_in, g_inR, hT, hTR, TO, B[6:8])),
            lambda: ssm_pass(TT, [(0, TO), (TO, TS)], True, "own"),
            attention_phase,
            mem_phase,
            merge_out_phase,
        ]
        for st_ in steps[:PHASE_LIMIT]:
            st_()
        S.emit(st)
    return nc


_NC = None


def kernel(**inp):
    global _NC
    f32 = lambda a: np.ascontiguousarray(np.asarray(a, dtype=np.float32))
    if _NC is None:
        _NC = build_program()
    nc = _NC
    xp = f32(inp["x_prompt"]); xs = f32(inp["x_sample"]); mp = f32(inp["mem_prompt"])
    shared = {
        "norm_in": f32(inp["norm_in"][0]), "w_in": f32(inp["w_in"][0]), "conv_w": f32(inp["conv_w"][0]),
        "conv_b": f32(inp["conv_b"][0]), "dt_bias": f32(inp["dt_bias"][0]), "a_log": f32(inp["a_log"][0]),
        "d_skip": f32(inp["d_skip"][0]), "ssm_norm": f32(inp["ssm_norm"][0]), "norm_mem": f32(inp["norm_mem"][0]),
        "w_mem_kv": f32(inp["w_mem_kv"][0]), "w_pa": f32(inp["w_pa"][0]), "w_pb": f32(inp["w_pb"][0]),
        "w_pm": f32(inp["w_pm"][0]), "w_o": f32(inp["w_o"][0]), "norm_final": f32(inp["norm_final"]),
    }
    in_maps = []
    for c in range(8):
        b, half = c // 2, c % 2
        fl = np.zeros((128, 2), np.float32)
        if half == 1:
            fl[:, 0] = 1.0
            xc = xp[b, 0:TC]
        else:
            fl[:, 1] = NEG
            xc = np.zeros((TC, D), np.float32)
        m = dict(shared)
        m.update({
            "x_own": f32(xp[b, half * TO:(half + 1) * TO]), "x_ctx": f32(xc), "x_smp": f32(xs[c]), "memx": f32(mp[b]),
            "flags": fl,
            "ck": f32(inp["cache_attn_k"][0, c].reshape(1024, 256)), "cv": f32(inp["cache_attn_v"][0, c].reshape(1024, 256)),
            "cki": f32(inp["cache_idx_k"][0, c]), "sconv": f32(inp["state_conv"][0, c]),
            "sssm": f32(inp["state_ssm"][0, c].reshape(2048, 128)),
            "cmk": f32(inp["cache_mem_k"][0, c].reshape(NM, 1024)), "cmv": f32(inp["cache_mem_v"][0, c].reshape(NM, 1024)),
        })
        in_maps.append(m)
    res = run_bass_kernel_spmd(nc, in_maps, core_ids=list(range(8)))
    R = res.results
    if DEBUG:
        kernel.dbg = R
    cat2 = lambda key, shp: np.stack([np.concatenate([R[2 * b][key], R[2 * b + 1][key]], axis=0) for b in range(4)]).reshape(shp)
    y_prompt = cat2("y_own", (4, 2048, D))
    y_sample = np.stack([R[c]["y_smp"] for c in range(8)])
    attn_k_p = cat2("k_own", (1, 4, 2048, 2, 128))
    attn_v_p = cat2("v_own", (1, 4, 2048, 2, 128))
    idx_k_p = cat2("ki_own", (1, 4, 2048, 64))
    conv_p = np.stack([R[2 * b + 1]["conv_p"] for b in range(4)]).reshape(1, 4, 3, 3072)
    ssm_p = np.stack([R[2 * b + 1]["ssm_p"] for b in range(4)]).reshape(1, 4, 32, 64, 128)
    mem_k_p = np.stack([R[2 * b]["memk_o"] for b in range(4)]).reshape(1, 4, NM, 4, 256)
    mem_v_p = np.stack([R[2 * b]["memv_o"] for b in range(4)]).reshape(1, 4, NM, 4, 256)
    attn_k_s = np.stack([R[c]["k_s"] for c in range(8)]).reshape(1, 8, TS, 2, 128)
    attn_v_s = np.stack([R[c]["v_s"] for c in range(8)]).reshape(1, 8, TS, 2, 128)
    idx_k_s = np.stack([R[c]["ki_s"] for c in range(8)]).reshape(1, 8, TS, 64)
    conv_s = np.stack([R[c]["conv_s"] for c in range(8)]).reshape(1, 8, 3, 3072)
    ssm_s = np.stack([R[c]["ssm_s"] for c in range(8)]).reshape(1, 8, 32, 64, 128)
    outs = (y_prompt, y_sample, attn_k_p, attn_v_p, idx_k_p, conv_p, ssm_p, mem_k_p, mem_v_p,
            attn_k_s, attn_v_s, idx_k_s, conv_s, ssm_s)
    return tuple(np.ascontiguousarray(o, dtype=np.float32) for o in outs)
```
